# Optimizing a Trainium2 kernel written in Bass

```python
import math
import jax, jax.numpy as jnp
from jax import lax
import numpy as np

D_MODEL = 2048
BATCH = 4
SEQ = 8192
DEPTH = 2

POOL_WINDOWS = (2, 4, 8, 16)
POOL_GROUPS = 4
POOL_GROUP_DIM = D_MODEL // 16
POOL_WIDTH = POOL_GROUPS * POOL_GROUP_DIM
SB_HEAD_DIM = 128
SB_HEADS = D_MODEL // 256
SB_WIDTH = SB_HEADS * SB_HEAD_DIM
SB_BLOCK = 128
GLA_HEADS = 4
GLA_DK = D_MODEL // 16
GLA_DV = D_MODEL // 16
GLA_KEY_WIDTH = GLA_HEADS * GLA_DK
GLA_WIDTH = GLA_HEADS * GLA_DV
GLA_RANK = 16
GLA_TAU = 16.0
GLA_CHUNK = 64
N_BRANCH = 3
D_FF = ((8 * D_MODEL // 3 + 255) // 256) * 256
RMS_EPS = 1e-6
IN_WIDTHS = (POOL_WIDTH, SB_WIDTH, SB_WIDTH, SB_WIDTH,
             GLA_KEY_WIDTH, GLA_KEY_WIDTH, GLA_WIDTH, GLA_WIDTH, GLA_RANK,
             N_BRANCH * D_MODEL)
IN_COLS = POOL_WIDTH + 3 * SB_WIDTH + 2 * GLA_KEY_WIDTH + 2 * GLA_WIDTH + GLA_RANK + N_BRANCH * D_MODEL

kernel_name = "hybrid_pool_stickbreak_gla_gated"


def _rmsnorm(x, gain, eps=RMS_EPS):
    xf = x.astype(jnp.float32)
    y = xf * lax.rsqrt(jnp.mean(xf * xf, axis=-1, keepdims=True) + eps)
    return (y * gain.astype(jnp.float32)).astype(x.dtype)


def _split_cols(t, widths):
    outs, start = [], 0
    for w in widths:
        outs.append(t[..., start:start + w])
        start += w
    return outs


def _pool_mixer(u, w_pool, pool_scale):
    b, s, _ = u.shape
    ug = u.astype(jnp.float32).reshape(b, s, POOL_GROUPS, POOL_GROUP_DIM)
    cs = jnp.cumsum(ug, axis=1)
    t1 = jnp.arange(1, s + 1, dtype=jnp.float32)[None, :, None]
    groups = []
    for g, w in enumerate(POOL_WINDOWS):
        csg = cs[:, :, g]
        cs_shift = jnp.pad(csg, ((0, 0), (w, 0), (0, 0)))[:, :s]
        count = jnp.minimum(t1, float(w))
        groups.append((csg - cs_shift) / count - ug[:, :, g])
    pooled = jnp.stack(groups, axis=2)
    y = jnp.einsum('bsgc,gcd->bsgd', pooled.astype(u.dtype), w_pool) * pool_scale
    return y.reshape(b, s, POOL_WIDTH)


def _stick_breaking(q, k, v, q_gain, k_gain):
    b, s, _ = q.shape

    def heads(t):
        return t.reshape(b, s, SB_HEADS, SB_HEAD_DIM).transpose(0, 2, 1, 3)

    qh = _rmsnorm(heads(q), q_gain)
    kh = _rmsnorm(heads(k), k_gain)
    vh = heads(v)
    nb = s // SB_BLOCK
    q_blocks = qh.reshape(b, SB_HEADS, nb, SB_BLOCK, SB_HEAD_DIM).transpose(2, 0, 1, 3, 4)
    key_pos = jnp.arange(s)
    inv_sqrt_d = 1.0 / math.sqrt(SB_HEAD_DIM)

    def block(args):
        i, qb = args
        z = jnp.einsum('bhqd,bhkd->bhqk', qb, kh,
                       preferred_element_type=jnp.float32) * inv_sqrt_d
        q_pos = i * SB_BLOCK + jnp.arange(SB_BLOCK)
        mask = key_pos[None, :] < q_pos[:, None]
        log_beta = jax.nn.log_sigmoid(z)
        log_keep = jnp.where(mask, jax.nn.log_sigmoid(-z), 0.0)
        after = lax.cumsum(log_keep, axis=3, reverse=True) - log_keep
        a = jnp.where(mask, jnp.exp(log_beta + after), 0.0)
        return jnp.einsum('bhqk,bhkd->bhqd', a.astype(vh.dtype), vh)

    out = lax.map(block, (jnp.arange(nb), q_blocks))
    return out.transpose(1, 0, 3, 2, 4).reshape(b, s, SB_WIDTH)


def _gla(q, k, v, r, a_low, w_a2, b_a2, out_gain):
    b, s, _ = q.shape
    nc = s // GLA_CHUNK
    log_alpha = jax.nn.log_sigmoid((a_low @ w_a2 + b_a2).astype(jnp.float32)) / GLA_TAU

    def chunks(t, d):
        return t.astype(jnp.float32).reshape(b, nc, GLA_CHUNK, GLA_HEADS, d).transpose(1, 0, 3, 2, 4)

    qc = chunks(q, GLA_DK) * (GLA_DK ** -0.5)
    kc = chunks(k, GLA_DK)
    vc = chunks(v, GLA_DV)
    gc = chunks(log_alpha, GLA_DK)
    causal = jnp.tril(jnp.ones((GLA_CHUNK, GLA_CHUNK), dtype=bool))

    def step(state, inp):
        qi, ki, vi, gi = inp
        bcum = jnp.cumsum(gi, axis=2)
        o_inter = jnp.einsum('bhtd,bhde->bhte', qi * jnp.exp(bcum), state)
        diff = bcum[:, :, :, None, :] - bcum[:, :, None, :, :]
        decay = jnp.where(causal[:, :, None], jnp.exp(jnp.minimum(diff, 0.0)), 0.0)
        scores = jnp.einsum('bhtd,bhsd,bhtsd->bhts', qi, ki, decay)
        o_intra = jnp.einsum('bhts,bhse->bhte', scores, vi)
        b_last = bcum[:, :, -1, :]
        state = jnp.exp(b_last)[..., None] * state + jnp.einsum(
            'bhsd,bhse->bhde', ki * jnp.exp(b_last[:, :, None, :] - bcum), vi)
        return state, o_inter + o_intra

    state0 = jnp.zeros((b, GLA_HEADS, GLA_DK, GLA_DV), jnp.float32)
    _, o = lax.scan(step, state0, (qc, kc, vc, gc))
    o = o.transpose(1, 0, 3, 2, 4).reshape(b, s, GLA_HEADS, GLA_DV)
    o = _rmsnorm(o, out_gain).reshape(b, s, GLA_WIDTH)
    return (o * jax.nn.silu(r.astype(jnp.float32))).astype(q.dtype)


def _mixer(h, w_in, w_pool, pool_scale, sb_q_gain, sb_k_gain, gla_w_a2, gla_b_a2,
           gla_out_gain, w_br_pool, w_br_sb, w_br_gla, w_out):
    b, s, _ = h.shape
    proj = h @ w_in
    (u_pool, q_sb, k_sb, v_sb, q_gla, k_gla, v_gla, r_gla, a_gla,
     gate_logits) = _split_cols(proj, IN_WIDTHS)
    y_pool = _pool_mixer(u_pool, w_pool, pool_scale)
    y_sb = _stick_breaking(q_sb, k_sb, v_sb, sb_q_gain, sb_k_gain)
    y_gla = _gla(q_gla, k_gla, v_gla, r_gla, a_gla, gla_w_a2, gla_b_a2, gla_out_gain)
    gates = jax.nn.sigmoid(gate_logits.astype(jnp.float32)).reshape(b, s, N_BRANCH, D_MODEL)
    merged = (gates[:, :, 0] * (y_pool @ w_br_pool)
              + gates[:, :, 1] * (y_sb @ w_br_sb)
              + gates[:, :, 2] * (y_gla @ w_br_gla))
    return merged.astype(h.dtype) @ w_out


def _swiglu(h, w_gate, w_up, w_down):
    return (jax.nn.silu(h @ w_gate) * (h @ w_up)) @ w_down


def setup_inputs(seed: int = 0) -> dict:
    key = jax.random.key(seed)
    ks = jax.random.split(key, 24)
    f32 = jnp.float32

    def nrm(k, shape, scale):
        return jax.random.normal(k, shape, f32) * scale

    def gain(k, shape):
        return 1.0 + 0.02 * jax.random.normal(k, shape, f32)

    L = DEPTH
    return {
        "x": nrm(ks[0], (BATCH, SEQ, D_MODEL), 1.0),
        "c": nrm(ks[1], (BATCH, D_MODEL), 1.0),
        "w_ada": nrm(ks[2], (L, D_MODEL, 6 * D_MODEL), 0.5 * D_MODEL ** -0.5),
        "b_ada": nrm(ks[3], (L, 6 * D_MODEL), 0.01),
        "g_norm1": gain(ks[4], (L, D_MODEL)),
        "w_in": nrm(ks[5], (L, D_MODEL, IN_COLS), D_MODEL ** -0.5),
        "w_pool": nrm(ks[6], (L, POOL_GROUPS, POOL_GROUP_DIM, POOL_GROUP_DIM), POOL_GROUP_DIM ** -0.5),
        "pool_scale": gain(ks[7], (L, POOL_GROUPS, POOL_GROUP_DIM)),
        "sb_q_gain": gain(ks[8], (L, SB_HEAD_DIM)),
        "sb_k_gain": gain(ks[9], (L, SB_HEAD_DIM)),
        "gla_w_a2": nrm(ks[10], (L, GLA_RANK, GLA_KEY_WIDTH), GLA_RANK ** -0.5),
        "gla_b_a2": nrm(ks[11], (L, GLA_KEY_WIDTH), 0.1),
        "gla_out_gain": gain(ks[12], (L, GLA_DV)),
        "w_br_pool": nrm(ks[13], (L, POOL_WIDTH, D_MODEL), POOL_WIDTH ** -0.5),
        "w_br_sb": nrm(ks[14], (L, SB_WIDTH, D_MODEL), SB_WIDTH ** -0.5),
        "w_br_gla": nrm(ks[15], (L, GLA_WIDTH, D_MODEL), GLA_WIDTH ** -0.5),
        "w_out": nrm(ks[16], (L, D_MODEL, D_MODEL), D_MODEL ** -0.5),
        "g_norm2": gain(ks[17], (L, D_MODEL)),
        "w_ff_gate": nrm(ks[18], (L, D_MODEL, D_FF), D_MODEL ** -0.5),
        "w_ff_up": nrm(ks[19], (L, D_MODEL, D_FF), D_MODEL ** -0.5),
        "w_ff_down": nrm(ks[20], (L, D_FF, D_MODEL), D_FF ** -0.5),
    }


def reference(x, c, w_ada, b_ada, g_norm1, w_in, w_pool, pool_scale, sb_q_gain, sb_k_gain,
              gla_w_a2, gla_b_a2, gla_out_gain, w_br_pool, w_br_sb, w_br_gla, w_out,
              g_norm2, w_ff_gate, w_ff_up, w_ff_down):
    silu_c = jax.nn.silu(c)
    for l in range(DEPTH):
        mod = (silu_c @ w_ada[l] + b_ada[l])[:, None, :]
        sh1, sc1, ga1, sh2, sc2, ga2 = jnp.split(mod, 6, axis=-1)
        h = _rmsnorm(x, g_norm1[l]) * (1.0 + sc1) + sh1
        x = x + ga1 * _mixer(h, w_in[l], w_pool[l], pool_scale[l], sb_q_gain[l], sb_k_gain[l],
                             gla_w_a2[l], gla_b_a2[l], gla_out_gain[l],
                             w_br_pool[l], w_br_sb[l], w_br_gla[l], w_out[l])
        h = _rmsnorm(x, g_norm2[l]) * (1.0 + sc2) + sh2
        x = x + ga2 * _swiglu(h, w_ff_gate[l], w_ff_up[l], w_ff_down[l])
    return x
```

```python
import numpy as np
import ml_dtypes
from contextlib import ExitStack
import concourse.bass as bass
import concourse.mybir as mybir
from concourse.bass_utils import run_bass_kernel_spmd

F32 = mybir.dt.float32
BF16 = mybir.dt.bfloat16
AF = mybir.ActivationFunctionType
ALU = mybir.AluOpType
NPBF = ml_dtypes.bfloat16

D = 2048
KD = 16
DFF = 5632
KF = 44
INC = 11792
TT = 512
EPS = 1e-6
NEG = -30000.0

C_U, C_SQ, C_SK, C_SV, C_GQ, C_GK, C_GV, C_GR, C_GA, C_GATE = 0, 512, 1536, 2560, 3584, 4096, 4608, 5120, 5632, 5648


class Buf:
    __slots__ = ("name", "w", "r")

    def __init__(self, name=""):
        self.name = name
        self.w = []
        self.r = {}


class KB:
    SELF_WAIT = ("act", "dve", "pool")

    def __init__(self, nc, es, n_dma_sems=12):
        self.nc = nc
        self.es = es
        self.engs = {"pe": nc.tensor, "act": nc.scalar, "dve": nc.vector, "pool": nc.gpsimd, "sp": nc.sync}
        self.semh = {}
        self.cnt = {}
        self.seen = {e: {} for e in self.engs}
        for e in self.engs:
            self.semh[e] = es.enter_context(nc.semaphore("sem_" + e))
            self.cnt[e] = 0
        self.dma_pool = {}
        for q in ("sp", "pool", "act"):
            keys = []
            for j in range(n_dma_sems):
                k = "dq_%s_%d" % (q, j)
                self.semh[k] = es.enter_context(nc.semaphore(k))
                self.cnt[k] = 0
                keys.append(k)
            self.dma_pool[q] = [keys, 0]
        self.n_inst = 0

    def _wait(self, eng, tok):
        key, val = tok
        if key == eng and eng not in self.SELF_WAIT:
            return
        if self.seen[eng].get(key, 0) >= val:
            return
        self.engs[eng].wait_ge(self.semh[key], val)
        self.seen[eng][key] = val

    def _deps(self, eng, reads, writes):
        for b in reads:
            for t in b.w:
                self._wait(eng, t)
        for b in writes:
            for t in b.w:
                self._wait(eng, t)
            for k, v in b.r.items():
                self._wait(eng, (k, v))

    def _commit(self, tok, reads, writes):
        for b in reads:
            if b.r.get(tok[0], 0) < tok[1]:
                b.r[tok[0]] = tok[1]
        for b in writes:
            b.w = [tok]
            b.r = {}

    def op(self, eng, fn, reads=(), writes=()):
        self._deps(eng, reads, writes)
        ins = fn(self.engs[eng])
        self.cnt[eng] += 1
        ins.then_inc(self.semh[eng], 1)
        tok = (eng, self.cnt[eng])
        self._commit(tok, reads, writes)
        self.n_inst += 1
        return tok

    def dma(self, q, out, in_, reads=(), writes=(), add_write=False):
        self._deps(q, reads, writes)
        keys, idx = self.dma_pool[q]
        k = keys[idx % len(keys)]
        self.dma_pool[q][1] = idx + 1
        if self.cnt[k] > 0:
            self._wait(q, (k, self.cnt[k]))
        ins = self.engs[q].dma_start(out=out, in_=in_)
        self.cnt[k] += 16
        ins.then_inc(self.semh[k], 16)
        tok = (k, self.cnt[k])
        for b in reads:
            if b.r.get(tok[0], 0) < tok[1]:
                b.r[tok[0]] = tok[1]
        for b in writes:
            if add_write:
                b.w = b.w + [tok]
            else:
                b.w = [tok]
                b.r = {}
        self.n_inst += 1
        return tok

    def barrier(self):
        for e in self.engs:
            for k, v in self.cnt.items():
                if v > 0 and k != e:
                    self._wait(e, (k, v))

    def finish(self, bufs):
        for b in bufs:
            for t in b.w:
                self._wait("sp", t)
        self.barrier()


class Ctx:
    pass


def sb(kb, es, name, shape, dt):
    return es.enter_context(kb.nc.sbuf_tensor(name, list(shape), dt))


def ps(kb, es, name, shape, dt):
    return es.enter_context(kb.nc.psum_tensor(name, list(shape), dt))


def emit_consts(kb, es, dr, need):
    C = {}

    def ld(name, shape, dt):
        t = sb(kb, es, "c_" + name, shape, dt)
        b = Buf(name)
        kb.dma("sp", t[tuple(slice(None) for _ in shape)], dr[name], writes=[b])
        C[name] = (t, b)

    ld("ident", [128, 128], BF16)
    ld("ones", [128, 128], BF16)
    if "A" in need or "C" in need:
        ld("identf", [128, 128], F32)
    if "A" in need:
        ld("jrev", [128, 128], BF16)
        ld("reset", [128, TT], F32)
        ld("gq", [128, 1], F32)
        ld("gk", [128, 1], F32)
        ld("wa2", [16, 512], F32)
        ld("ba2", [128, 4], F32)
    if "B" in need:
        ld("sbmask", [128, 128], BF16)
        ld("glamask", [128, 128], F32)
        ld("onecol", [128, TT + 1], F32)
        ld("zeros", [128, TT + 1], F32)
        ld("gog", [128, 1], F32)
        ld("wpool", [128, 2, 128], F32)
        ld("pscale", [128, 2], F32)
        ld("pcoef", [128, 2, 4], F32)
        ld("pinv", [128, 2, 16], F32)
    return C


def emit_mod(kb, es, dr, l, C):
    nc = kb.nc
    ccol = sb(kb, es, "ccol%d" % l, [128, KD], F32)
    scb = sb(kb, es, "scb%d" % l, [128, KD], BF16)
    bada = sb(kb, es, "bada%d" % l, [128, 96], F32)
    g1 = sb(kb, es, "g1_%d" % l, [128, KD], F32)
    g2 = sb(kb, es, "g2_%d" % l, [128, KD], F32)
    mod = sb(kb, es, "mod%d" % l, [128, 96], F32)
    gs = sb(kb, es, "gs%d" % l, [128, 2 * KD], F32)
    bmod = Buf("mod")
    bc = Buf("c")
    kb.dma("sp", ccol[:, :], dr["c_col"], writes=[bc])
    kb.dma("sp", bada[:, :], dr["b_ada%d" % l], writes=[bc], add_write=True)
    kb.dma("sp", g1[:, :], dr["g_norm1_%d" % l], writes=[bc], add_write=True)
    kb.dma("sp", g2[:, :], dr["g_norm2_%d" % l], writes=[bc], add_write=True)
    bsc = Buf("sc")
    kb.op("act", lambda e: e.activation(out=scb[:, :], in_=ccol[:, :], func=AF.Silu), reads=[bc], writes=[bsc])
    with ExitStack() as es2:
        slabs = [sb(kb, es2, "adaslab%d_%d" % (l, i), [128, KD, 512], BF16) for i in range(2)]
        bsl = [Buf("adaslab%d" % i) for i in range(2)]
        mps = ps(kb, es2, "modps%d" % l, [128, 96], F32)
        bps = Buf("modps")
        wada = dr["w_ada%d" % l]
        for s in range(24):
            sl, bs = slabs[s % 2], bsl[s % 2]
            kb.dma("pool", sl[:, :, :], wada[:, 512 * s:512 * s + 512].rearrange("(kc p) c -> p kc c", p=128),
                   writes=[bs])

            def f(e, s=s, sl=sl):
                ins = None
                for j in range(4):
                    col = 4 * s + j
                    for k in range(KD):
                        ins = e.matmul(mps[:, col:col + 1], lhsT=sl[:, k, 128 * j:128 * j + 128], rhs=scb[:, k:k + 1],
                                       start=(k == 0), stop=(k == KD - 1))
                return ins
            kb.op("pe", f, reads=[bs, bsc], writes=[bps])
        kb.op("dve", lambda e: e.tensor_tensor(out=mod[:, :], in0=mps[:, :], in1=bada[:, :], op=ALU.add),
              reads=[bps, bc], writes=[bmod])
        kb.barrier()
    kb.op("dve", lambda e: e.scalar_tensor_tensor(out=gs[:, 0:16], in0=mod[:, 16:32], scalar=1.0, in1=g1[:, :],
                                                  op0=ALU.add, op1=ALU.mult), reads=[bmod, bc], writes=[bmod])
    kb.op("dve", lambda e: e.scalar_tensor_tensor(out=gs[:, 16:32], in0=mod[:, 64:80], scalar=1.0, in1=g2[:, :],
                                                  op0=ALU.add, op1=ALU.mult), reads=[bmod, bc], writes=[bmod])
    M = Ctx()
    M.buf = bmod
    M.sh1 = mod[:, 0:16]
    M.gs1 = gs[:, 0:16]
    M.ga1 = mod[:, 32:48]
    M.sh2 = mod[:, 48:64]
    M.gs2 = gs[:, 16:32]
    M.ga2 = mod[:, 80:96]
    return M


def emit_wcast(kb, dr, src, dst, rows, c0, c1, rchunk=256, r_dst0=0, first=True):
    buf = dr["_b_" + dst]
    for r0 in range(0, rows, rchunk):
        r1 = min(rows, r0 + rchunk)
        kb.dma("pool", dr[dst][r_dst0 + r0:r_dst0 + r1, c0:c1], dr[src][r0:r1, c0:c1], writes=[buf], add_write=not first)
        first = False


def emit_norm(kb, N, xT, bx, hT, bh, gs, sh, M, C):
    ones, bones = C["ones"]
    for k in range(KD):
        sq, bsq = N.sq[k % 2], N.bsq[k % 2]
        kb.op("act", lambda e, k=k, sq=sq: e.activation(out=sq[:, :], in_=xT[:, k, :], func=AF.Square),
              reads=[bx[k]], writes=[bsq])
        kb.op("pe", lambda e, k=k, sq=sq: e.matmul(N.ssps[:, :], lhsT=ones[:, :], rhs=sq[:, :], start=(k == 0),
                                                    stop=(k == KD - 1)), reads=[bsq, bones], writes=[N.bssps])
    kb.op("act", lambda e: e.activation(out=N.rstd[:, :], in_=N.ssps[:, :], func=AF.Sqrt, scale=1.0 / D, bias=N.epsc[:, 0:1]),
          reads=[N.bssps, N.beps], writes=[N.brstd])
    kb.op("dve", lambda e: e.reciprocal(out=N.rstd[:, :], in_=N.rstd[:, :]), reads=[N.brstd], writes=[N.brstd])
    for k in range(KD):
        tmp, btmp = N.tmp[k % 2], N.btmp[k % 2]
        kb.op("dve", lambda e, k=k, tmp=tmp: e.scalar_tensor_tensor(out=tmp[:, :], in0=xT[:, k, :], scalar=gs[:, k:k + 1],
                                                                    in1=N.rstd[:, :], op0=ALU.mult, op1=ALU.mult),
              reads=[bx[k], N.brstd, M.buf], writes=[btmp])
        kb.op("act", lambda e, k=k, tmp=tmp: e.activation(out=hT[:, k, :], in_=tmp[:, :], func=AF.Identity,
                                                          bias=sh[:, k:k + 1], scale=1.0),
              reads=[btmp, M.buf], writes=[bh[k]])


def alloc_norm(kb, es):
    N = Ctx()
    N.sq = [sb(kb, es, "nsq%d" % i, [128, TT], BF16) for i in range(2)]
    N.bsq = [Buf() for _ in range(2)]
    N.tmp = [sb(kb, es, "ntmp%d" % i, [128, TT], F32) for i in range(2)]
    N.btmp = [Buf() for _ in range(2)]
    N.rstd = sb(kb, es, "nrstd", [128, TT], F32)
    N.brstd = Buf()
    N.ssps = ps(kb, es, "nssps", [128, TT], F32)
    N.bssps = Buf()
    N.epsc = sb(kb, es, "nepsc", [128, 1], F32)
    N.beps = Buf()
    kb.op("dve", lambda e: e.memset(N.epsc[:, :], EPS), writes=[N.beps])
    return N


def emit_phaseC(kb, dr, l, cfg, C, M, last):
    nc = kb.nc
    TOK = cfg["TOK"]
    NT = TOK // TT
    with ExitStack() as es:
        xT = sb(kb, es, "c_xT", [128, KD, TT], F32)
        bx = [Buf("xT%d" % k) for k in range(KD)]
        hT = sb(kb, es, "c_hT", [128, KD, TT], BF16)
        bh = [Buf() for _ in range(KD)]
        yT = sb(kb, es, "c_yT", [128, KD, TT], BF16)
        by = Buf("yT")
        mT = sb(kb, es, "c_mT", [128, KD, TT], BF16)
        bm = [Buf() for _ in range(KD)]
        aT = sb(kb, es, "c_aT", [128, KF, TT], BF16)
        ba = [Buf() for _ in range(KF)]
        NS = 3
        slab = [sb(kb, es, "c_slab%d" % i, [128, KF, 128], BF16) for i in range(NS)]
        bslab = [Buf() for _ in range(NS)]
        sig = [sb(kb, es, "c_sig%d" % i, [128, TT], F32) for i in range(3)]
        bsig = [Buf() for _ in range(3)]
        macc = [sb(kb, es, "c_macc%d" % i, [128, TT], F32) for i in range(2)]
        bmacc = [Buf() for _ in range(2)]
        sg = [sb(kb, es, "c_sg%d" % i, [128, TT], F32) for i in range(2)]
        bsg = [Buf() for _ in range(2)]
        N = alloc_norm(kb, es)
        NPS = 6
        pp = [ps(kb, es, "c_ps%d" % i, [128, TT], F32) for i in range(NPS)]
        bpp = [Buf() for _ in range(NPS)]
        if last:
            ot = [sb(kb, es, "c_ot%d" % i, [128, D], F32) for i in range(2)]
            bot = [Buf() for _ in range(2)]
            identf, bidf = C["identf"]
        st = {"slab": 0, "ps": 0}

        def next_slab():
            i = st["slab"] % NS
            st["slab"] += 1
            return slab[i], bslab[i]

        def next_ps():
            i = st["ps"] % NPS
            st["ps"] += 1
            return pp[i], bpp[i]

        def load_slab(wname, nk, c0, ncols=128):
            sl, bs = next_slab()
            kb.dma("sp", sl[:, 0:nk, 0:ncols], dr[wname][:, c0:c0 + ncols].rearrange("(kc p) c -> p kc c", p=128),
                   reads=[dr["_b_" + wname]], writes=[bs])
            return sl, bs

        def mm_group(pt, bpt, sl, bs, nk, rhs, brhs):
            def f(e):
                ins = None
                for k in range(nk):
                    ins = e.matmul(pt[:, :], lhsT=sl[:, k, 0:128], rhs=rhs[:, k, :], start=(k == 0), stop=(k == nk - 1))
                return ins
            kb.op("pe", f, reads=[bs] + list(brhs), writes=[bpt])

        for ti in range(NT):
            t0 = ti * TT
            kb.dma("sp", xT[:, :, :], dr["xT"][:, t0:t0 + TT].rearrange("(kc p) t -> p kc t", p=128),
                   reads=[dr["_b_xT"]], writes=bx)
            kb.dma("sp", yT[:, :, :], dr["yT"][:, t0:t0 + TT].rearrange("(kc p) t -> p kc t", p=128),
                   reads=[dr["_b_yT"]], writes=[by])
            emit_norm(kb, N, xT, bx, hT, bh, M.gs1, M.sh1, M, C)
            for fc in range(KD):
                gate_ps = []
                for i in range(3):
                    sl, bs = load_slab("w_in_bf", KD, C_GATE + i * D + fc * 128)
                    pt, bpt = next_ps()
                    mm_group(pt, bpt, sl, bs, KD, hT, bh)
                    kb.op("act", lambda e, i=i, pt=pt: e.activation(out=sig[i][:, :], in_=pt[:, :], func=AF.Sigmoid),
                          reads=[bpt], writes=[bsig[i]])
                sl, bs = load_slab("w_br_bf", KD, fc * 128)
                mi = fc % 2
                for i, (k0, k1) in enumerate(((0, 4), (4, 12), (12, 16))):
                    pt, bpt = next_ps()

                    def f(e, pt=pt, sl=sl, k0=k0, k1=k1):
                        ins = None
                        for k in range(k0, k1):
                            ins = e.matmul(pt[:, :], lhsT=sl[:, k, 0:128], rhs=yT[:, k, :], start=(k == k0), stop=(k == k1 - 1))
                        return ins
                    kb.op("pe", f, reads=[bs, by], writes=[bpt])
                    if i == 0:
                        kb.op("dve", lambda e, pt=pt: e.tensor_tensor(out=macc[mi][:, :], in0=pt[:, :], in1=sig[0][:, :], op=ALU.mult),
                              reads=[bpt, bsig[0]], writes=[bmacc[mi]])
                    else:
                        sgi = i - 1
                        kb.op("dve", lambda e, pt=pt, i=i, sgi=sgi: e.tensor_tensor(out=sg[sgi][:, :], in0=pt[:, :], in1=sig[i][:, :], op=ALU.mult),
                              reads=[bpt, bsig[i]], writes=[bsg[sgi]])
                        if i == 1:
                            kb.op("pool", lambda e, sgi=sgi: e.tensor_tensor(out=macc[mi][:, :], in0=macc[mi][:, :], in1=sg[sgi][:, :], op=ALU.add),
                                  reads=[bsg[sgi], bmacc[mi]], writes=[bmacc[mi]])
                        else:
                            kb.op("pool", lambda e, sgi=sgi, fc=fc: e.tensor_tensor(out=mT[:, fc, :], in0=macc[mi][:, :], in1=sg[sgi][:, :], op=ALU.add),
                                  reads=[bsg[sgi], bmacc[mi]], writes=[bm[fc]])
            for fc in range(KD):
                sl, bs = load_slab("w_out_bf", KD, fc * 128)
                pt, bpt = next_ps()
                mm_group(pt, bpt, sl, bs, KD, mT, bm)
                kb.op("dve", lambda e, pt=pt, fc=fc: e.scalar_tensor_tensor(out=xT[:, fc, :], in0=pt[:, :], scalar=M.ga1[:, fc:fc + 1],
                                                                            in1=xT[:, fc, :], op0=ALU.mult, op1=ALU.add),
                      reads=[bpt, M.buf, bx[fc]], writes=[bx[fc]])
            emit_norm(kb, N, xT, bx, hT, bh, M.gs2, M.sh2, M, C)
            for j in range(KF):
                slg, bsg_ = load_slab("w_gate_bf", KD, j * 128)
                slu, bsu = load_slab("w_up_bf", KD, j * 128)
                pg, bpg = next_ps()
                pu, bpu = next_ps()
                mm_group(pg, bpg, slg, bsg_, KD, hT, bh)
                mm_group(pu, bpu, slu, bsu, KD, hT, bh)
                si = j % 2
                kb.op("act", lambda e, pg=pg, si=si: e.activation(out=sg[si][:, :], in_=pg[:, :], func=AF.Silu),
                      reads=[bpg], writes=[bsg[si]])
                kb.op("dve", lambda e, pu=pu, si=si, j=j: e.tensor_tensor(out=aT[:, j, :], in0=pu[:, :], in1=sg[si][:, :], op=ALU.mult),
                      reads=[bpu, bsg[si]], writes=[ba[j]])
            for fc in range(KD):
                sl, bs = load_slab("w_down_bf", KF, fc * 128)
                pt, bpt = next_ps()
                mm_group(pt, bpt, sl, bs, KF, aT, ba)
                kb.op("dve", lambda e, pt=pt, fc=fc: e.scalar_tensor_tensor(out=xT[:, fc, :], in0=pt[:, :], scalar=M.ga2[:, fc:fc + 1],
                                                                            in1=xT[:, fc, :], op0=ALU.mult, op1=ALU.add),
                      reads=[bpt, M.buf, bx[fc]], writes=[bx[fc]])
            if not last:
                kb.dma("pool", dr["xT_out"][:, t0:t0 + TT].rearrange("(kc p) t -> p kc t", p=128), xT[:, :, :],
                       reads=bx, writes=[dr["_b_xT_out"]], add_write=True)
            else:
                for b in range(4):
                    o, bo = ot[b % 2], bot[b % 2]
                    for kq in range(4):
                        pt, bpt = next_ps()

                        def f(e, pt=pt, kq=kq, b=b):
                            ins = None
                            for kk in range(4):
                                k = 4 * kq + kk
                                ins = e.transpose(pt[:, 128 * kk:128 * kk + 128], xT[:, k, 128 * b:128 * b + 128], identf[:, :])
                            return ins
                        kb.op("pe", f, reads=bx + [bidf], writes=[bpt])
                        eng = "act" if kq % 2 == 0 else "dve"
                        if eng == "act":
                            kb.op("act", lambda e, pt=pt, kq=kq, o=o: e.activation(out=o[:, 512 * kq:512 * kq + 512], in_=pt[:, :], func=AF.Copy),
                                  reads=[bpt], writes=[bo])
                        else:
                            kb.op("dve", lambda e, pt=pt, kq=kq, o=o: e.tensor_copy(out=o[:, 512 * kq:512 * kq + 512], in_=pt[:, :]),
                                  reads=[bpt], writes=[bo])
                    kb.dma("pool", dr["out"][t0 + 128 * b:t0 + 128 * b + 128, :], o[:, :], reads=[bo],
                           writes=[dr["_b_out"]], add_write=True)
        kb.barrier()


def emit_phaseA(kb, dr, l, cfg, C, M, first):
    TOK = cfg["TOK"]
    NT = TOK // TT
    NCH = TOK // 128
    ident, bid = C["ident"]
    jrev, bjr = C["jrev"]
    with ExitStack() as es:
        xT = sb(kb, es, "a_xT", [128, KD, TT], F32)
        bx = [Buf() for _ in range(KD)]
        hT = sb(kb, es, "a_hT", [128, KD, TT], BF16)
        bh = [Buf() for _ in range(KD)]
        if first:
            xin = [sb(kb, es, "a_xin%d" % i, [128, D], F32) for i in range(4)]
            bxin = [Buf() for _ in range(4)]
            identf, bidf = C["identf"]
        slab = [sb(kb, es, "a_slab%d" % i, [128, KD, 512], BF16) for i in range(2)]
        bslab = [Buf() for _ in range(2)]
        N = alloc_norm(kb, es)
        NPS = 5
        pp = [ps(kb, es, "a_ps%d" % i, [128, TT], F32) for i in range(NPS)]
        bpp = [Buf() for _ in range(NPS)]
        NSTG = 6
        stg = [sb(kb, es, "a_stg%d" % i, [128, TT], BF16) for i in range(NSTG)]
        bstg = [Buf() for _ in range(NSTG)]
        qn = [sb(kb, es, "a_qn%d" % i, [128, TT], BF16) for i in range(4)]
        bqn = [Buf() for _ in range(4)]
        junk = sb(kb, es, "a_junk", [128, 128], BF16)
        bjunk = Buf()
        ss = sb(kb, es, "a_ss", [128, 16], F32)
        bss = Buf()
        rs = sb(kb, es, "a_rs", [128, 16], F32)
        brs = Buf()
        wa2b = sb(kb, es, "a_wa2b", [16, 512], BF16)
        nba2 = sb(kb, es, "a_nba2", [128, 4], F32)
        gqs = sb(kb, es, "a_gqs", [128, 1], F32)
        bset = Buf()
        alow = sb(kb, es, "a_alow", [16, TT], BF16)
        balow = Buf()
        g1t = sb(kb, es, "a_g1t", [128, TT], F32)
        g2t = sb(kb, es, "a_g2t", [128, TT], F32)
        cum = sb(kb, es, "a_cum", [128, TT], F32)
        bg1, bg2, bcum = Buf(), Buf(), Buf()
        nb = sb(kb, es, "a_nb", [128, 4], F32)
        bnb = Buf()
        EB = [[sb(kb, es, "a_eb%d_%d" % (h, i), [128, TT], F32) for i in range(3)] for h in range(4)]
        bEB = [[Buf() for i in range(3)] for h in range(4)]
        ebt = sb(kb, es, "a_ebt", [128, 4, NCH], F32)
        bebt = Buf()
        khT = sb(kb, es, "a_khT", [128, TT], BF16)
        bkhT = Buf()
        vst = [sb(kb, es, "a_vst%d" % i, [128, TT], BF16) for i in range(2)]
        bvst = [Buf() for _ in range(2)]
        st = {"slab": 0, "ps": 0, "stg": 0, "ev": 0}

        wa2, bwa2 = C["wa2"]
        ba2, bba2 = C["ba2"]
        gq, bgq = C["gq"]
        gk, bgk = C["gk"]
        reset, breset = C["reset"]
        kb.op("dve", lambda e: e.tensor_copy(out=wa2b[:, :], in_=wa2[:, :]), reads=[bwa2], writes=[bset])
        kb.op("dve", lambda e: e.tensor_scalar(out=nba2[:, :], in0=ba2[:, :], scalar1=-1.0, scalar2=None, op0=ALU.mult),
              reads=[bba2], writes=[bset])
        kb.op("dve", lambda e: e.tensor_scalar(out=gqs[:, :], in0=gq[:, :], scalar1=128.0 ** -0.5, scalar2=None, op0=ALU.mult),
              reads=[bgq], writes=[bset])

        def next_ps():
            i = st["ps"] % NPS
            st["ps"] += 1
            return pp[i], bpp[i]

        def next_stg():
            i = st["stg"] % NSTG
            st["stg"] += 1
            return stg[i], bstg[i]

        def load_slab(c0, ncols):
            i = st["slab"] % 2
            st["slab"] += 1
            sl, bs = slab[i], bslab[i]
            kb.dma("sp", sl[:, :, 0:ncols], dr["w_in_bf"][:, c0:c0 + ncols].rearrange("(kc p) c -> p kc c", p=128),
                   reads=[dr["_b_w_in_bf"]], writes=[bs])
            return sl, bs

        def fm_chunk(sl, bs, j, m=128):
            pt, bpt = next_ps()

            def f(e):
                ins = None
                for k in range(KD):
                    ins = e.matmul(pt[0:m, :], lhsT=sl[:, k, 128 * j:128 * j + m], rhs=hT[:, k, :], start=(k == 0), stop=(k == KD - 1))
                return ins
            kb.op("pe", f, reads=[bs] + bh, writes=[bpt])
            return pt, bpt

        def tm_block(sl, bs, b):
            pt, bpt = next_ps()

            def f(e):
                ins = None
                for k in range(KD):
                    ins = e.matmul(pt[:, :], lhsT=hT[:, k, 128 * b:128 * b + 128], rhs=sl[:, k, :], start=(k == 0), stop=(k == KD - 1))
                return ins
            kb.op("pe", f, reads=[bs] + bh, writes=[bpt])
            return pt, bpt

        def evac(out_ap, in_ap, reads, writes, scale=None, sreads=()):
            st["ev"] += 1
            if st["ev"] % 2 == 0:
                if scale is None:
                    kb.op("act", lambda e: e.activation(out=out_ap, in_=in_ap, func=AF.Copy), reads=reads, writes=writes)
                else:
                    kb.op("act", lambda e: e.activation(out=out_ap, in_=in_ap, func=AF.Copy, scale=scale),
                          reads=list(reads) + list(sreads), writes=writes)
            else:
                if scale is None:
                    kb.op("dve", lambda e: e.tensor_copy(out=out_ap, in_=in_ap), reads=reads, writes=writes)
                else:
                    kb.op("dve", lambda e: e.tensor_scalar(out=out_ap, in0=in_ap, scalar1=scale, scalar2=None, op0=ALU.mult),
                          reads=list(reads) + list(sreads), writes=writes)

        def store(dst_ap, src_ap, bsrc, bdst):
            kb.dma("pool", dst_ap, src_ap, reads=(bsrc if isinstance(bsrc, list) else [bsrc]), writes=[bdst], add_write=True)

        for ti in range(NT):
            t0 = ti * TT
            if first:
                for b in range(4):
                    kb.dma("sp", xin[b][:, :], dr["x_tok"][t0 + 128 * b:t0 + 128 * b + 128, :], writes=[bxin[b]])
                for k in range(KD):
                    pt, bpt = next_ps()

                    def f(e, pt=pt, k=k):
                        ins = None
                        for b in range(4):
                            ins = e.transpose(pt[:, 128 * b:128 * b + 128], xin[b][:, 128 * k:128 * k + 128], identf[:, :])
                        return ins
                    kb.op("pe", f, reads=bxin + [bidf], writes=[bpt])
                    evac(xT[:, k, :], pt[:, :], [bpt], [bx[k]])
                store(dr["xT_out"][:, t0:t0 + TT].rearrange("(kc p) t -> p kc t", p=128), xT[:, :, :], bx, dr["_b_xT_out"])
            else:
                kb.dma("sp", xT[:, :, :], dr["xT"][:, t0:t0 + TT].rearrange("(kc p) t -> p kc t", p=128),
                       reads=[dr["_b_xT"]], writes=bx)
            emit_norm(kb, N, xT, bx, hT, bh, M.gs1, M.sh1, M, C)

            sl, bs = load_slab(C_GA, 16)
            pt, bpt = fm_chunk(sl, bs, 0, m=16)
            kb.op("dve", lambda e, pt=pt: e.tensor_copy(out=alow[:, :], in_=pt[0:16, :]), reads=[bpt], writes=[balow])
            for hd in range(4):
                zp, bzp = next_ps()
                kb.op("pe", lambda e, zp=zp, hd=hd: e.matmul(zp[:, :], lhsT=wa2b[0:16, 128 * hd:128 * hd + 128], rhs=alow[0:16, :],
                                                             start=True, stop=True), reads=[balow, bset], writes=[bzp])
                kb.op("act", lambda e, zp=zp, hd=hd: e.activation(out=g1t[:, :], in_=zp[:, :], func=AF.Exp, scale=-1.0,
                                                                  bias=nba2[:, hd:hd + 1]), reads=[bzp, bset], writes=[bg1])
                kb.op("act", lambda e: e.activation(out=g2t[:, :], in_=g1t[:, :], func=AF.Ln, bias=1.0, scale=1.0),
                      reads=[bg1], writes=[bg2])
                kb.op("dve", lambda e: e.tensor_tensor_scan(out=cum[:, :], data0=reset[:, :], data1=g2t[:, :], initial=0.0,
                                                            op0=ALU.mult, op1=ALU.add), reads=[bg2, breset], writes=[bcum])
                kb.op("act", lambda e, hd=hd: e.activation(out=EB[hd][0][:, :], in_=cum[:, :], func=AF.Exp, scale=-1.0 / 16),
                      reads=[bcum], writes=[bEB[hd][0]])
                kb.op("act", lambda e, hd=hd: e.activation(out=EB[hd][1][:, :], in_=cum[:, :], func=AF.Exp, scale=1.0 / 16),
                      reads=[bcum], writes=[bEB[hd][1]])
                kb.op("dve", lambda e: e.tensor_scalar(out=nb[:, :], in0=cum[:, :].rearrange("p (c t) -> p c t", t=128)[:, :, 127],
                                                       scalar1=-1.0 / 16, scalar2=None, op0=ALU.mult), reads=[bcum], writes=[bnb])
                for c in range(4):
                    kb.op("act", lambda e, hd=hd, c=c: e.activation(out=EB[hd][2][:, 128 * c:128 * c + 128], in_=cum[:, 128 * c:128 * c + 128],
                                                                    func=AF.Exp, scale=1.0 / 16, bias=nb[:, c:c + 1]),
                          reads=[bcum, bnb], writes=[bEB[hd][2]])
                kb.op("act", lambda e, hd=hd: e.activation(out=ebt[:, hd, 4 * ti:4 * ti + 4], in_=nb[:, :], func=AF.Exp),
                      reads=[bnb], writes=[bebt])
            sl, bs = load_slab(C_GQ, 512)
            for hd in range(4):
                pt, bpt = fm_chunk(sl, bs, hd)
                sg_, bsg_ = next_stg()
                kb.op("dve", lambda e, pt=pt, hd=hd, sg_=sg_: e.scalar_tensor_tensor(out=sg_[:, :], in0=pt[:, :], scalar=128.0 ** -0.5,
                                                                                   in1=EB[hd][0][:, :], op0=ALU.mult, op1=ALU.mult),
                      reads=[bpt, bEB[hd][0]], writes=[bsg_])
                store(dr["gqT"][128 * hd:128 * hd + 128, t0:t0 + TT], sg_[:, :], bsg_, dr["_b_gqT"])
            sl, bs = load_slab(C_GK, 512)
            for hd in range(4):
                pt, bpt = fm_chunk(sl, bs, hd)
                sg_, bsg_ = next_stg()
                kb.op("dve", lambda e, pt=pt, hd=hd, sg_=sg_: e.tensor_tensor(out=sg_[:, :], in0=pt[:, :], in1=EB[hd][1][:, :], op=ALU.mult),
                      reads=[bpt, bEB[hd][1]], writes=[bsg_])
                store(dr["gkT"][128 * hd:128 * hd + 128, t0:t0 + TT], sg_[:, :], bsg_, dr["_b_gkT"])
                kb.op("dve", lambda e, pt=pt, hd=hd: e.tensor_tensor(out=khT[:, :], in0=pt[:, :], in1=EB[hd][2][:, :], op=ALU.mult),
                      reads=[bpt, bEB[hd][2]], writes=[bkhT])
                p2, bp2 = next_ps()

                def f(e, p2=p2):
                    ins = None
                    for b in range(4):
                        ins = e.matmul(p2[:, 128 * b:128 * b + 128], lhsT=khT[:, 128 * b:128 * b + 128], rhs=ident[:, :], start=True, stop=True)
                    return ins
                kb.op("pe", f, reads=[bkhT, bid], writes=[bp2])
                sg2, bsg2 = next_stg()
                evac(sg2[:, :], p2[:, :], [bp2], [bsg2])
                store(dr["gkh"][t0:t0 + TT, 128 * hd:128 * hd + 128].rearrange("(b p) c -> p b c", p=128),
                      sg2[:, :].rearrange("p (b c) -> p b c", c=128), bsg2, dr["_b_gkh"])
            sl, bs = load_slab(C_GR, 512)
            for hd in range(4):
                pt, bpt = fm_chunk(sl, bs, hd)
                sg_, bsg_ = next_stg()
                kb.op("act", lambda e, pt=pt, sg_=sg_: e.activation(out=sg_[:, :], in_=pt[:, :], func=AF.Silu), reads=[bpt], writes=[bsg_])
                store(dr["grT"][128 * hd:128 * hd + 128, t0:t0 + TT], sg_[:, :], bsg_, dr["_b_grT"])
            sl, bs = load_slab(C_GV, 512)
            for b in range(4):
                pt, bpt = tm_block(sl, bs, b)
                sg_, bsg_ = next_stg()
                evac(sg_[:, :], pt[:, :], [bpt], [bsg_])
                store(dr["gv"][t0 + 128 * b:t0 + 128 * b + 128, :], sg_[:, :], bsg_, dr["_b_gv"])
            sl, bs = load_slab(C_U, 512)
            for j in range(4):
                pt, bpt = fm_chunk(sl, bs, j)
                sg_, bsg_ = next_stg()
                evac(sg_[:, :], pt[:, :], [bpt], [bsg_])
                store(dr["uT"][128 * j:128 * j + 128, t0:t0 + TT], sg_[:, :], bsg_, dr["_b_uT"])
            for which, c0, gain, bgain, perm, bperm, dname in (("q", C_SQ, gqs, bset, ident, bid, "sbqT"),
                                                               ("k", C_SK, gk, bgk, jrev, bjr, "sbkT")):
                for half in range(2):
                    sl, bs = load_slab(c0 + 512 * half, 512)
                    for b in range(4):
                        pt, bpt = tm_block(sl, bs, b)
                        for hh in range(4):
                            kb.op("act", lambda e, pt=pt, hh=hh, b=b: e.activation(out=junk[:, :], in_=pt[:, 128 * hh:128 * hh + 128], func=AF.Square,
                                                                                    accum_out=ss[:, 4 * b + hh:4 * b + hh + 1]),
                                  reads=[bpt], writes=[bjunk, bss])
                        kb.op("act", lambda e, b=b: e.activation(out=rs[:, 4 * b:4 * b + 4], in_=ss[:, 4 * b:4 * b + 4], func=AF.Sqrt, scale=1.0 / 128,
                                                                 bias=N.epsc[:, 0:1]), reads=[bss, N.beps], writes=[brs])
                        kb.op("dve", lambda e, b=b: e.reciprocal(out=rs[:, 4 * b:4 * b + 4], in_=rs[:, 4 * b:4 * b + 4]), reads=[brs], writes=[brs])
                        for hh in range(4):
                            evac(qn[b][:, 128 * hh:128 * hh + 128], pt[:, 128 * hh:128 * hh + 128], [bpt], [bqn[b]],
                                 scale=rs[:, 4 * b + hh:4 * b + hh + 1], sreads=[brs])
                    for hh in range(4):
                        p2, bp2 = next_ps()

                        def f(e, p2=p2, hh=hh, perm=perm, which=which):
                            ins = None
                            for b in range(4):
                                bb = b if which == "q" else 3 - b
                                ins = e.matmul(p2[:, 128 * bb:128 * bb + 128], lhsT=qn[b][:, 128 * hh:128 * hh + 128], rhs=perm[:, :],
                                               start=True, stop=True)
                            return ins
                        kb.op("pe", f, reads=bqn + [bperm], writes=[bp2])
                        sg_, bsg_ = next_stg()
                        evac(sg_[:, :], p2[:, :], [bp2], [bsg_], scale=gain[:, 0:1], sreads=[bgain])
                        head = 4 * half + hh
                        if which == "q":
                            store(dr[dname][128 * head:128 * head + 128, t0:t0 + TT], sg_[:, :], bsg_, dr["_b_" + dname])
                        else:
                            store(dr[dname][128 * head:128 * head + 128, TOK - t0 - TT:TOK - t0], sg_[:, :], bsg_, dr["_b_" + dname])
            for half in range(2):
                sl, bs = load_slab(C_SV + 512 * half, 512)
                for b in range(4):
                    pt, bpt = tm_block(sl, bs, b)
                    v_, bv_ = vst[b % 2], bvst[b % 2]
                    evac(v_[:, :], pt[:, :], [bpt], [bv_])
                    p2, bp2 = next_ps()
                    kb.op("pe", lambda e, p2=p2, v_=v_: e.matmul(p2[:, :], lhsT=jrev[:, :], rhs=v_[:, :], start=True, stop=True),
                          reads=[bv_, bjr], writes=[bp2])
                    sg_, bsg_ = next_stg()
                    evac(sg_[:, :], p2[:, :], [bp2], [bsg_])
                    r0 = TOK - (t0 + 128 * b) - 128
                    store(dr["sbv"][r0:r0 + 128, 512 * half:512 * half + 512], sg_[:, :], bsg_, dr["_b_sbv"])
        store(dr["geb"][:, :, :], ebt[:, :, :], bebt, dr["_b_geb"])
        kb.barrier()


def emit_phaseB_pool(kb, dr, cfg, C):
    S = cfg["S"]
    NTL = S // TT
    W = TT + 16
    wpool, bwp = C["wpool"]
    pscale, bpsc = C["pscale"]
    pcoef, bpc = C["pcoef"]
    pinv, bpi = C["pinv"]
    with ExitStack() as es:
        wpb = sb(kb, es, "p_wpb", [128, 2, 128], BF16)
        bwpb = Buf()
        kb.op("dve", lambda e: e.tensor_copy(out=wpb[:, :, :], in_=wpool[:, :, :]), reads=[bwp], writes=[bwpb])
        ub = [sb(kb, es, "p_ub%d" % i, [128, W], BF16) for i in range(2)]
        bub = [Buf() for _ in range(2)]
        U = sb(kb, es, "p_U", [128, W], F32)
        S1 = sb(kb, es, "p_S1", [128, W], F32)
        S2 = sb(kb, es, "p_S2", [128, W], F32)
        S3 = sb(kb, es, "p_S3", [128, W], F32)
        S4 = sb(kb, es, "p_S4", [128, W], F32)
        acc = sb(kb, es, "p_acc", [128, W], F32)
        bw = Buf()
        pl = [sb(kb, es, "p_pl%d" % i, [128, TT], BF16) for i in range(2)]
        bpl = [Buf() for _ in range(2)]
        stg = [sb(kb, es, "p_stg%d" % i, [128, TT], BF16) for i in range(2)]
        bstg = [Buf() for _ in range(2)]
        pp = [ps(kb, es, "p_ps%d" % i, [128, TT], F32) for i in range(2)]
        bpp = [Buf() for _ in range(2)]
        n = 0
        for g in range(2):
            for j in range(NTL):
                u_, bu_ = ub[n % 2], bub[n % 2]
                if j == 0:
                    kb.op("dve", lambda e, u_=u_: e.memset(u_[:, 0:16], 0.0), writes=[bu_])
                    kb.dma("sp", u_[:, 16:W], dr["b_uT"][128 * g:128 * g + 128, 0:TT], reads=[dr["_b_b_uT"]], writes=[bu_], add_write=True)
                else:
                    kb.dma("sp", u_[:, 0:W], dr["b_uT"][128 * g:128 * g + 128, TT * j - 16:TT * j + TT], reads=[dr["_b_b_uT"]], writes=[bu_])
                kb.op("dve", lambda e, u_=u_: e.tensor_copy(out=U[:, :], in_=u_[:, :]), reads=[bu_], writes=[bw])
                kb.op("dve", lambda e: e.tensor_tensor(out=S1[:, 1:W], in0=U[:, 1:W], in1=U[:, 0:W - 1], op=ALU.add), reads=[bw], writes=[bw])
                kb.op("dve", lambda e: e.tensor_tensor(out=S2[:, 3:W], in0=S1[:, 3:W], in1=S1[:, 1:W - 2], op=ALU.add), reads=[bw], writes=[bw])
                kb.op("dve", lambda e: e.tensor_tensor(out=S3[:, 7:W], in0=S2[:, 7:W], in1=S2[:, 3:W - 4], op=ALU.add), reads=[bw], writes=[bw])
                kb.op("dve", lambda e: e.tensor_tensor(out=S4[:, 15:W], in0=S3[:, 15:W], in1=S3[:, 7:W - 8], op=ALU.add), reads=[bw], writes=[bw])
                kb.op("dve", lambda e, g=g: e.tensor_scalar(out=acc[:, 16:W], in0=S1[:, 16:W], scalar1=pcoef[:, g, 0:1], scalar2=None, op0=ALU.mult),
                      reads=[bw, bpc], writes=[bw])
                for lv, SL in ((1, S2), (2, S3), (3, S4)):
                    kb.op("dve", lambda e, g=g, lv=lv, SL=SL: e.scalar_tensor_tensor(out=acc[:, 16:W], in0=SL[:, 16:W], scalar=pcoef[:, g, lv:lv + 1],
                                                                                    in1=acc[:, 16:W], op0=ALU.mult, op1=ALU.add),
                          reads=[bw, bpc], writes=[bw])
                if j == 0:
                    kb.op("dve", lambda e, g=g: e.tensor_tensor(out=acc[:, 16:32], in0=acc[:, 16:32], in1=pinv[:, g, :], op=ALU.mult),
                          reads=[bw, bpi], writes=[bw])
                p_, bp_ = pl[n % 2], bpl[n % 2]
                kb.op("dve", lambda e, p_=p_: e.tensor_tensor(out=p_[:, :], in0=acc[:, 16:W], in1=U[:, 16:W], op=ALU.subtract),
                      reads=[bw], writes=[bp_])
                pt, bpt = pp[n % 2], bpp[n % 2]
                kb.op("pe", lambda e, pt=pt, p_=p_, g=g: e.matmul(pt[:, :], lhsT=wpb[:, g, :], rhs=p_[:, :], start=True, stop=True),
                      reads=[bp_, bwpb], writes=[bpt])
                s_, bs_ = stg[n % 2], bstg[n % 2]
                kb.op("act", lambda e, pt=pt, s_=s_, g=g: e.activation(out=s_[:, :], in_=pt[:, :], func=AF.Copy, scale=pscale[:, g:g + 1]),
                      reads=[bpt, bpsc], writes=[bs_])
                kb.dma("pool", dr["b_y"][128 * g:128 * g + 128, TT * j:TT * j + TT], s_[:, :], reads=[bs_], writes=[dr["_b_b_y"]], add_write=True)
                n += 1
        kb.barrier()


def emit_phaseB_gla(kb, dr, cfg, C):
    S = cfg["S"]
    NB = S // 128
    ones, bones = C["ones"]
    gmask, bgm = C["glamask"]
    gog, bgog = C["gog"]
    with ExitStack() as es:
        qT = sb(kb, es, "g_qT", [128, S], BF16)
        kT = sb(kb, es, "g_kT", [128, S], BF16)
        kh = sb(kb, es, "g_kh", [128, NB, 128], BF16)
        v = sb(kb, es, "g_v", [128, NB, 128], BF16)
        rT = sb(kb, es, "g_rT", [128, S], BF16)
        bin_ = Buf()
        eb = sb(kb, es, "g_eb", [128, 2, NB], F32)
        beb = Buf()
        kb.dma("sp", eb[:, :, :], dr["b_geb"], reads=[dr["_b_b_geb"]], writes=[beb])
        Sf = sb(kb, es, "g_Sf", [128, 128], F32)
        bSf = Buf()
        Sb = [sb(kb, es, "g_Sb%d" % i, [128, 128], BF16) for i in range(2)]
        bSb = [Buf() for _ in range(2)]
        scm = [sb(kb, es, "g_scm%d" % i, [128, 128], BF16) for i in range(2)]
        bscm = [Buf() for _ in range(2)]
        sq = sb(kb, es, "g_sq", [128, TT], BF16)
        bsq = Buf()
        rstd = sb(kb, es, "g_rstd", [128, TT], F32)
        brstd = Buf()
        t1 = sb(kb, es, "g_t1", [128, TT], F32)
        bt1 = Buf()
        epsc = sb(kb, es, "g_epsc", [128, 1], F32)
        beps = Buf()
        kb.op("dve", lambda e: e.memset(epsc[:, :], EPS), writes=[beps])
        stg = [sb(kb, es, "g_stg%d" % i, [128, TT], BF16) for i in range(2)]
        bstg = [Buf() for _ in range(2)]
        sps = [ps(kb, es, "g_sps%d" % i, [128, 128], F32) for i in range(2)]
        bsps = [Buf() for _ in range(2)]
        sp2 = [ps(kb, es, "g_sp2%d" % i, [128, 128], F32) for i in range(2)]
        bsp2 = [Buf() for _ in range(2)]
        ops = [ps(kb, es, "g_ops%d" % i, [128, TT], F32) for i in range(2)]
        bops = [Buf() for _ in range(2)]
        ssp = ps(kb, es, "g_ssp", [128, TT], F32)
        bssp = Buf()
        for hd in range(2):
            r0 = 128 * hd
            kb.dma("sp", qT[:, :], dr["b_gqT"][r0:r0 + 128, :], reads=[dr["_b_b_gqT"]], writes=[bin_])
            kb.dma("sp", kT[:, :], dr["b_gkT"][r0:r0 + 128, :], reads=[dr["_b_b_gkT"]], writes=[bin_], add_write=True)
            kb.dma("sp", rT[:, :], dr["b_grT"][r0:r0 + 128, :], reads=[dr["_b_b_grT"]], writes=[bin_], add_write=True)
            kb.dma("sp", kh[:, :, :], dr["b_gkh"][:, r0:r0 + 128].rearrange("(c p) d -> p c d", p=128), reads=[dr["_b_b_gkh"]],
                   writes=[bin_], add_write=True)
            kb.dma("sp", v[:, :, :], dr["b_gv"][:, r0:r0 + 128].rearrange("(c p) d -> p c d", p=128), reads=[dr["_b_b_gv"]],
                   writes=[bin_], add_write=True)
            kb.op("dve", lambda e: e.memset(Sf[:, :], 0.0), writes=[bSf])
            kb.op("dve", lambda e: e.memset(Sb[0][:, :], 0.0), writes=[bSb[0]])
            for c in range(NB):
                blk = slice(128 * c, 128 * c + 128)
                sp_, bsp_ = sps[c % 2], bsps[c % 2]
                kb.op("pe", lambda e, sp_=sp_, blk=blk: e.matmul(sp_[:, :], lhsT=kT[:, blk], rhs=qT[:, blk], start=True, stop=True),
                      reads=[bin_], writes=[bsp_])
                sc_, bsc_ = scm[c % 2], bscm[c % 2]
                kb.op("dve", lambda e, sp_=sp_, sc_=sc_: e.tensor_tensor(out=sc_[:, :], in0=sp_[:, :], in1=gmask[:, :], op=ALU.mult),
                      reads=[bsp_, bgm], writes=[bsc_])
                op_, bop_ = ops[(c // 4) % 2], bops[(c // 4) % 2]
                oc = slice(128 * (c % 4), 128 * (c % 4) + 128)

                def f(e, op_=op_, oc=oc, blk=blk, c=c, sc_=sc_):
                    e.matmul(op_[:, oc], lhsT=Sb[c % 2][:, :], rhs=qT[:, blk], start=True, stop=False)
                    return e.matmul(op_[:, oc], lhsT=v[:, c, :], rhs=sc_[:, :], start=False, stop=True)
                kb.op("pe", f, reads=[bSb[c % 2], bin_, bsc_], writes=[bop_])
                s2_, bs2_ = sp2[c % 2], bsp2[c % 2]
                kb.op("pe", lambda e, s2_=s2_, c=c: e.matmul(s2_[:, :], lhsT=kh[:, c, :], rhs=v[:, c, :], start=True, stop=True),
                      reads=[bin_], writes=[bs2_])
                kb.op("dve", lambda e, s2_=s2_, c=c, hd=hd: e.scalar_tensor_tensor(out=Sf[:, :], in0=Sf[:, :], scalar=eb[:, hd, c:c + 1], in1=s2_[:, :],
                                                                                 op0=ALU.mult, op1=ALU.add),
                      reads=[bs2_, beb, bSf], writes=[bSf])
                kb.op("act", lambda e, c=c: e.activation(out=Sb[(c + 1) % 2][:, :], in_=Sf[:, :], func=AF.Copy),
                      reads=[bSf], writes=[bSb[(c + 1) % 2]])
                if c % 4 == 3:
                    tg = c // 4
                    kb.op("act", lambda e, op_=op_: e.activation(out=sq[:, :], in_=op_[:, :], func=AF.Square), reads=[bop_], writes=[bsq])
                    kb.op("pe", lambda e: e.matmul(ssp[:, :], lhsT=ones[:, :], rhs=sq[:, :], start=True, stop=True), reads=[bsq, bones], writes=[bssp])
                    kb.op("act", lambda e: e.activation(out=rstd[:, :], in_=ssp[:, :], func=AF.Sqrt, scale=1.0 / 128, bias=epsc[:, 0:1]),
                          reads=[bssp, beps], writes=[brstd])
                    kb.op("dve", lambda e: e.reciprocal(out=rstd[:, :], in_=rstd[:, :]), reads=[brstd], writes=[brstd])
                    kb.op("dve", lambda e, op_=op_: e.tensor_tensor(out=t1[:, :], in0=op_[:, :], in1=rstd[:, :], op=ALU.mult),
                          reads=[bop_, brstd], writes=[bt1])
                    s_, bs_ = stg[tg % 2], bstg[tg % 2]
                    kb.op("dve", lambda e, s_=s_, tg=tg: e.scalar_tensor_tensor(out=s_[:, :], in0=t1[:, :], scalar=gog[:, 0:1], in1=rT[:, TT * tg:TT * tg + TT],
                                                                                op0=ALU.mult, op1=ALU.mult),
                          reads=[bt1, bgog, bin_], writes=[bs_])
                    kb.dma("pool", dr["b_y"][768 + r0:768 + r0 + 128, TT * tg:TT * tg + TT], s_[:, :], reads=[bs_], writes=[dr["_b_b_y"]], add_write=True)
        kb.barrier()


def emit_phaseB_sb(kb, dr, cfg, C):
    S = cfg["S"]
    NB = S // 128
    ident, bid = C["ident"]
    sbmask, bmask = C["sbmask"]
    zeros, bzer = C["zeros"]
    LAG = 2
    with ExitStack() as es:
        qT = [sb(kb, es, "s_qT%d" % i, [128, S], BF16) for i in range(2)]
        kT = [sb(kb, es, "s_kT%d" % i, [128, S], BF16) for i in range(2)]
        vv = [sb(kb, es, "s_v%d" % i, [128, NB, 128], BF16) for i in range(2)]
        bin_ = [Buf() for _ in range(2)]
        NZ, NA = 3, 4
        zps = [ps(kb, es, "s_zps%d" % i, [128, TT], F32) for i in range(NZ)]
        bzps = [Buf() for _ in range(NZ)]
        tps = [ps(kb, es, "s_tps%d" % i, [128, TT], F32) for i in range(2)]
        btps = [Buf() for _ in range(2)]
        ops = [ps(kb, es, "s_ops%d" % i, [128, 128], F32) for i in range(2)]
        bops = [Buf() for _ in range(2)]
        Bt = [sb(kb, es, "s_Bt%d" % i, [128, TT], F32) for i in range(2)]
        bBt = [Buf() for _ in range(2)]
        KBf = [sb(kb, es, "s_KB%d" % i, [128, TT + 1], F32) for i in range(2)]
        bKB = [Buf() for _ in range(2)]
        PX = [sb(kb, es, "s_PX%d" % i, [128, TT + 1], F32) for i in range(3)]
        bPX = [Buf() for _ in range(3)]
        A = [sb(kb, es, "s_A%d" % i, [128, TT], BF16) for i in range(NA)]
        bA = [Buf() for _ in range(NA)]
        AT = [sb(kb, es, "s_AT%d" % i, [128, TT], BF16) for i in range(2)]
        bAT = [Buf() for _ in range(2)]
        ost = [sb(kb, es, "s_ost%d" % i, [128, TT], BF16) for i in range(2)]
        bost = [Buf() for _ in range(2)]
        for i in range(2):
            kb.op("dve", lambda e, i=i: e.memset(KBf[i][:, 0:1], 1.0), writes=[bKB[i]])
        tiles = []
        for hd in range(4):
            for i in range(NB):
                nkb = i + 1
                rb0 = NB - 1 - i
                nt = (nkb + 3) // 4
                for m in range(nt):
                    tiles.append((hd, i, m, rb0 + 4 * m, min(4, nkb - 4 * m), m == 0, m == nt - 1))
        NTI = len(tiles)

        def load_head(hd):
            s_ = hd % 2
            r0 = 128 * hd
            kb.dma("sp", qT[s_][:, :], dr["b_sbqT"][r0:r0 + 128, :], reads=[dr["_b_b_sbqT"]], writes=[bin_[s_]])
            kb.dma("sp", kT[s_][:, :], dr["b_sbkT"][r0:r0 + 128, :], reads=[dr["_b_b_sbkT"]], writes=[bin_[s_]], add_write=True)
            kb.dma("sp", vv[s_][:, :, :], dr["b_sbv"][:, r0:r0 + 128].rearrange("(c p) d -> p c d", p=128), reads=[dr["_b_b_sbv"]],
                   writes=[bin_[s_]], add_write=True)

        def front(n):
            hd, i, m, rb, nb, first, last = tiles[n]
            s_ = hd % 2
            w = 128 * nb
            z, bz = zps[n % NZ], bzps[n % NZ]

            def f(e):
                ins = e.matmul(z[:, 0:w], lhsT=qT[s_][:, 128 * i:128 * i + 128], rhs=kT[s_][:, 128 * rb:128 * rb + w], start=True, stop=not first)
                if first:
                    ins = e.matmul(z[:, 0:128], lhsT=ident[:, :], rhs=sbmask[:, :], start=False, stop=True)
                return ins
            kb.op("pe", f, reads=[bin_[s_], bid, bmask], writes=[bz])
            b_, bb_ = Bt[n % 2], bBt[n % 2]
            k_, bk_ = KBf[n % 2], bKB[n % 2]
            kb.op("act", lambda e: e.activation(out=b_[:, 0:w], in_=z[:, 0:w], func=AF.Sigmoid), reads=[bz], writes=[bb_])
            kb.op("act", lambda e: e.activation(out=k_[:, 1:w + 1], in_=z[:, 0:w], func=AF.Sigmoid, scale=-1.0), reads=[bz], writes=[bk_])
            px, bpx = PX[n % 3], bPX[n % 3]
            if first:
                kb.op("dve", lambda e: e.tensor_tensor_scan(out=px[:, 0:w + 1], data0=k_[:, 0:w + 1], data1=zeros[:, 0:w + 1], initial=1.0,
                                                            op0=ALU.mult, op1=ALU.add), reads=[bk_, bzer], writes=[bpx])
            else:
                ppx, bppx = PX[(n - 1) % 3], bPX[(n - 1) % 3]
                kb.op("dve", lambda e: e.tensor_tensor_scan(out=px[:, 0:w + 1], data0=k_[:, 0:w + 1], data1=zeros[:, 0:w + 1],
                                                            initial=ppx[:, TT:TT + 1], op0=ALU.mult, op1=ALU.add),
                      reads=[bk_, bzer, bppx], writes=[bpx])
            a_, ba_ = A[n % NA], bA[n % NA]
            kb.op("pool", lambda e: e.tensor_tensor(out=a_[:, 0:w], in0=b_[:, 0:w], in1=px[:, 0:w], op=ALU.mult), reads=[bb_, bpx], writes=[ba_])

        def back(n):
            hd, i, m, rb, nb, first, last = tiles[n]
            s_ = hd % 2
            w = 128 * nb
            a_, ba_ = A[n % NA], bA[n % NA]
            tp, btp = tps[n % 2], btps[n % 2]

            def f(e):
                ins = None
                for j in range(nb):
                    ins = e.matmul(tp[:, 128 * j:128 * j + 128], lhsT=a_[:, 128 * j:128 * j + 128], rhs=ident[:, :], start=True, stop=True)
                return ins
            kb.op("pe", f, reads=[ba_, bid], writes=[btp])
            at, bat = AT[n % 2], bAT[n % 2]
            if n % 2 == 0:
                kb.op("act", lambda e: e.activation(out=at[:, 0:w], in_=tp[:, 0:w], func=AF.Copy), reads=[btp], writes=[bat])
            else:
                kb.op("dve", lambda e: e.tensor_copy(out=at[:, 0:w], in_=tp[:, 0:w]), reads=[btp], writes=[bat])
            qi = hd * NB + i
            o, bo = ops[qi % 2], bops[qi % 2]

            def g(e):
                ins = None
                for j in range(nb):
                    ins = e.matmul(o[:, :], lhsT=vv[s_][:, rb + j, :], rhs=at[:, 128 * j:128 * j + 128], start=(first and j == 0), stop=(last and j == nb - 1))
                return ins
            kb.op("pe", g, reads=[bat, bin_[s_]], writes=[bo])
            if last:
                os_, bos_ = ost[(qi // 4) % 2], bost[(qi // 4) % 2]
                kb.op("act", lambda e: e.activation(out=os_[:, 128 * (i % 4):128 * (i % 4) + 128], in_=o[:, :], func=AF.Copy), reads=[bo], writes=[bos_])
                if i % 4 == 3:
                    kb.dma("pool", dr["b_y"][256 + 128 * hd:256 + 128 * hd + 128, 128 * (i - 3):128 * (i - 3) + TT], os_[:, :], reads=[bos_],
                           writes=[dr["_b_b_y"]], add_write=True)

        load_head(0)
        loaded = 0
        for n in range(NTI + LAG):
            if n < NTI:
                hd = tiles[n][0]
                if hd > loaded:
                    loaded = hd
                if tiles[n][1] == 0 and tiles[n][2] == 0 and hd + 1 < 4:
                    pass
                front(n)
            if n - LAG >= 0:
                back(n - LAG)
                hdb, ib, mb = tiles[n - LAG][0], tiles[n - LAG][1], tiles[n - LAG][2]
                if tiles[n - LAG][6] and ib == NB - 1 and hdb + 2 < 4:
                    load_head(hdb + 2)
            if n == 0:
                load_head(1)
        kb.barrier()


class DR(dict):
    def add(self, nc, name, shape, dt, kind):
        self[name] = nc.dram_tensor(name, list(shape), dt, kind=kind).ap()
        self["_b_" + name] = Buf(name)


CONST_SHAPES = {
    "ident": ([128, 128], BF16), "ones": ([128, 128], BF16), "identf": ([128, 128], F32), "jrev": ([128, 128], BF16),
    "reset": ([128, TT], F32), "gq": ([128, 1], F32), "gk": ([128, 1], F32), "wa2": ([16, 512], F32), "ba2": ([128, 4], F32),
    "sbmask": ([128, 128], BF16), "glamask": ([128, 128], F32), "onecol": ([128, TT + 1], F32), "zeros": ([128, TT + 1], F32),
    "gog": ([128, 1], F32), "wpool": ([128, 2, 128], F32), "pscale": ([128, 2], F32), "pcoef": ([128, 2, 4], F32),
    "pinv": ([128, 2, 16], F32),
}
CONST_A = ["ident", "ones", "identf", "jrev", "reset", "gq", "gk", "wa2", "ba2"]
CONST_B = ["ident", "ones", "sbmask", "glamask", "onecol", "zeros", "gog", "wpool", "pscale", "pcoef", "pinv"]
CONST_C = ["ident", "ones", "identf"]


def decl_mod_inputs(nc, dr, l):
    dr.add(nc, "c_col", [128, KD], F32, "ExternalInput")
    dr.add(nc, "b_ada%d" % l, [128, 96], F32, "ExternalInput")
    dr.add(nc, "g_norm1_%d" % l, [128, KD], F32, "ExternalInput")
    dr.add(nc, "g_norm2_%d" % l, [128, KD], F32, "ExternalInput")
    dr.add(nc, "w_ada%d" % l, [D, 6 * D], F32, "ExternalInput")


A_OUTS = lambda TOK: {"uT": ([512, TOK], BF16), "sbqT": ([1024, TOK], BF16), "sbkT": ([1024, TOK], BF16), "sbv": ([TOK, 1024], BF16),
                      "gqT": ([512, TOK], BF16), "gkT": ([512, TOK], BF16), "gkh": ([TOK, 512], BF16), "gv": ([TOK, 512], BF16),
                      "grT": ([512, TOK], BF16), "geb": ([128, 4, TOK // 128], F32)}
B_INS = lambda S: {"b_uT": ([256, S], BF16), "b_sbqT": ([512, S], BF16), "b_sbkT": ([512, S], BF16), "b_sbv": ([S, 512], BF16),
                   "b_gqT": ([256, S], BF16), "b_gkT": ([256, S], BF16), "b_gkh": ([S, 256], BF16), "b_gv": ([S, 256], BF16),
                   "b_grT": ([256, S], BF16), "b_geb": ([128, 2, S // 128], F32)}


def build_A(l, cfg):
    TOK = cfg["TOK"]
    nc = bass.Bass("TRN2", target_bir_lowering=False)
    dr = DR()
    for n in CONST_A:
        dr.add(nc, n, CONST_SHAPES[n][0], CONST_SHAPES[n][1], "ExternalInput")
    decl_mod_inputs(nc, dr, l)
    dr.add(nc, "w_in", [D, INC], F32, "ExternalInput")
    dr.add(nc, "w_in_bf", [D, INC], BF16, "Internal")
    first = (l == 0)
    if first:
        dr.add(nc, "x_tok", [TOK, D], F32, "ExternalInput")
        dr.add(nc, "xT_out", [D, TOK], F32, "ExternalOutput")
    else:
        dr.add(nc, "xT", [D, TOK], F32, "ExternalInput")
    outs = A_OUTS(TOK)
    for n, (shp, dt) in outs.items():
        dr.add(nc, n, shp, dt, "ExternalOutput")
    with ExitStack() as es:
        kb = KB(nc, es)
        emit_wcast(kb, dr, "w_in", "w_in_bf", D, 0, C_GATE)
        C = emit_consts(kb, es, dr, "A")
        M = emit_mod(kb, es, dr, l, C)
        emit_phaseA(kb, dr, l, cfg, C, M, first)
        kb.finish([dr["_b_" + n] for n in list(outs) + (["xT_out"] if first else [])])
    return nc, list(outs) + (["xT_out"] if first else [])


def build_B(l, cfg):
    S = cfg["S"]
    nc = bass.Bass("TRN2", target_bir_lowering=False)
    dr = DR()
    for n in CONST_B:
        dr.add(nc, n, CONST_SHAPES[n][0], CONST_SHAPES[n][1], "ExternalInput")
    for n, (shp, dt) in B_INS(S).items():
        dr.add(nc, n, shp, dt, "ExternalInput")
    dr.add(nc, "b_y", [1024, S], BF16, "ExternalOutput")
    with ExitStack() as es:
        kb = KB(nc, es)
        C = emit_consts(kb, es, dr, "B")
        emit_phaseB_pool(kb, dr, cfg, C)
        emit_phaseB_gla(kb, dr, cfg, C)
        emit_phaseB_sb(kb, dr, cfg, C)
        kb.finish([dr["_b_b_y"]])
    return nc, ["b_y"]


def build_C(l, cfg, last):
    TOK = cfg["TOK"]
    nc = bass.Bass("TRN2", target_bir_lowering=False)
    dr = DR()
    for n in CONST_C:
        dr.add(nc, n, CONST_SHAPES[n][0], CONST_SHAPES[n][1], "ExternalInput")
    decl_mod_inputs(nc, dr, l)
    dr.add(nc, "w_in", [D, INC], F32, "ExternalInput")
    dr.add(nc, "w_in_bf", [D, INC], BF16, "Internal")
    dr.add(nc, "w_br_pool", [512, D], F32, "ExternalInput")
    dr.add(nc, "w_br_sb", [1024, D], F32, "ExternalInput")
    dr.add(nc, "w_br_gla", [512, D], F32, "ExternalInput")
    dr.add(nc, "w_br_bf", [D, D], BF16, "Internal")
    dr.add(nc, "w_out", [D, D], F32, "ExternalInput")
    dr.add(nc, "w_out_bf", [D, D], BF16, "Internal")
    dr.add(nc, "w_ff_gate", [D, DFF], F32, "ExternalInput")
    dr.add(nc, "w_gate_bf", [D, DFF], BF16, "Internal")
    dr.add(nc, "w_ff_up", [D, DFF], F32, "ExternalInput")
    dr.add(nc, "w_up_bf", [D, DFF], BF16, "Internal")
    dr.add(nc, "w_ff_down", [DFF, D], F32, "ExternalInput")
    dr.add(nc, "w_down_bf", [DFF, D], BF16, "Internal")
    dr.add(nc, "xT", [D, TOK], F32, "ExternalInput")
    dr.add(nc, "yT", [D, TOK], BF16, "ExternalInput")
    oname = "out" if last else "xT_out"
    if last:
        dr.add(nc, "out", [TOK, D], F32, "ExternalOutput")
    else:
        dr.add(nc, "xT_out", [D, TOK], F32, "ExternalOutput")
    with ExitStack() as es:
        kb = KB(nc, es)
        emit_wcast(kb, dr, "w_in", "w_in_bf", D, C_GATE, INC)
        emit_wcast(kb, dr, "w_br_pool", "w_br_bf", 512, 0, D, r_dst0=0)
        emit_wcast(kb, dr, "w_br_sb", "w_br_bf", 1024, 0, D, r_dst0=512, first=False)
        emit_wcast(kb, dr, "w_br_gla", "w_br_bf", 512, 0, D, r_dst0=1536, first=False)
        emit_wcast(kb, dr, "w_out", "w_out_bf", D, 0, D)
        emit_wcast(kb, dr, "w_ff_gate", "w_gate_bf", D, 0, DFF)
        emit_wcast(kb, dr, "w_ff_up", "w_up_bf", D, 0, DFF)
        emit_wcast(kb, dr, "w_ff_down", "w_down_bf", DFF, 0, D)
        C = emit_consts(kb, es, dr, "C")
        M = emit_mod(kb, es, dr, l, C)
        emit_phaseC(kb, dr, l, cfg, C, M, last)
        kb.finish([dr["_b_" + oname]])
    return nc, [oname]


POOL_WINDOWS = (2, 4, 8, 16)


def host_consts(inp, l, h):
    c = {}
    c["ident"] = np.eye(128, dtype=np.float32).astype(NPBF)
    c["ones"] = np.ones((128, 128), np.float32).astype(NPBF)
    c["identf"] = np.eye(128, dtype=np.float32)
    c["jrev"] = np.eye(128, dtype=np.float32)[::-1].copy().astype(NPBF)
    rs = np.ones((128, TT), np.float32)
    rs[:, ::128] = 0.0
    c["reset"] = rs
    c["gq"] = np.ascontiguousarray(inp["sb_q_gain"][l][:, None])
    c["gk"] = np.ascontiguousarray(inp["sb_k_gain"][l][:, None])
    c["wa2"] = np.ascontiguousarray(inp["gla_w_a2"][l])
    c["ba2"] = np.ascontiguousarray(inp["gla_b_a2"][l].reshape(4, 128).T)
    p = np.arange(128)[:, None]
    cc = np.arange(128)[None, :]
    c["sbmask"] = np.where(cc + p > 127, 0.0, NEG).astype(np.float32).astype(NPBF)
    c["glamask"] = (p <= cc).astype(np.float32)
    c["onecol"] = np.ones((128, TT + 1), np.float32)
    c["zeros"] = np.zeros((128, TT + 1), np.float32)
    c["gog"] = np.ascontiguousarray(inp["gla_out_gain"][l][:, None])
    wp = inp["w_pool"][l][2 * h:2 * h + 2]
    c["wpool"] = np.ascontiguousarray(wp.transpose(1, 0, 2))
    c["pscale"] = np.ascontiguousarray(inp["pool_scale"][l][2 * h:2 * h + 2].T)
    pc = np.zeros((128, 2, 4), np.float32)
    pi = np.ones((128, 2, 16), np.float32)
    for gi in range(2):
        w = POOL_WINDOWS[2 * h + gi]
        lv = {2: 0, 4: 1, 8: 2, 16: 3}[w]
        pc[:, gi, lv] = 1.0 / w
        for t in range(16):
            pi[:, gi, t] = float(w) / float(min(t + 1, w))
    c["pcoef"] = pc
    c["pinv"] = pi
    return c


def mod_inputs(inp, l, b):
    return {
        "c_col": np.ascontiguousarray(inp["c"][b].reshape(KD, 128).T),
        "b_ada%d" % l: np.ascontiguousarray(inp["b_ada"][l].reshape(96, 128).T),
        "g_norm1_%d" % l: np.ascontiguousarray(inp["g_norm1"][l].reshape(KD, 128).T),
        "g_norm2_%d" % l: np.ascontiguousarray(inp["g_norm2"][l].reshape(KD, 128).T),
        "w_ada%d" % l: inp["w_ada"][l],
    }


_PROG_CACHE = {}


def get_prog(key, fn):
    if key not in _PROG_CACHE:
        _PROG_CACHE[key] = fn()
    return _PROG_CACHE[key]


def run_module(inp, B, S):
    inp = {k: np.asarray(v) for k, v in inp.items()}
    TOK = S // 2
    cfg = {"S": S, "TOK": TOK}
    NCORE = 2 * B
    cores = list(range(NCORE))
    depth = inp["w_in"].shape[0]
    xT = [None] * NCORE
    out = np.zeros((B, S, D), np.float32)
    for l in range(depth):
        nc, onames = get_prog(("A", l == 0, S), lambda: build_A(l, cfg)) if False else build_A(l, cfg)
        maps = []
        for c in cores:
            b, h = c // 2, c % 2
            m = {n: v for n, v in host_consts(inp, l, h).items() if n in CONST_A}
            m.update(mod_inputs(inp, l, b))
            m["w_in"] = inp["w_in"][l]
            if l == 0:
                m["x_tok"] = np.ascontiguousarray(inp["x"][b, h * TOK:(h + 1) * TOK])
            else:
                m["xT"] = xT[c]
            maps.append(m)
        res = run_bass_kernel_spmd(nc, maps, core_ids=cores).results
        if l == 0:
            xT = [res[c]["xT_out"] for c in cores]
        nc, _ = build_B(l, cfg)
        maps = []
        for c in cores:
            b, h = c // 2, c % 2
            r0, r1 = res[2 * b], res[2 * b + 1]
            m = {n: v for n, v in host_consts(inp, l, h).items() if n in CONST_B}
            m["b_uT"] = np.concatenate([r0["uT"][256 * h:256 * h + 256], r1["uT"][256 * h:256 * h + 256]], axis=1)
            m["b_sbqT"] = np.concatenate([r0["sbqT"][512 * h:512 * h + 512], r1["sbqT"][512 * h:512 * h + 512]], axis=1)
            m["b_sbkT"] = np.concatenate([r1["sbkT"][512 * h:512 * h + 512], r0["sbkT"][512 * h:512 * h + 512]], axis=1)
            m["b_sbv"] = np.concatenate([r1["sbv"][:, 512 * h:512 * h + 512], r0["sbv"][:, 512 * h:512 * h + 512]], axis=0)
            for nm in ("gqT", "gkT", "grT"):
                m["b_" + nm] = np.concatenate([r0[nm][256 * h:256 * h + 256], r1[nm][256 * h:256 * h + 256]], axis=1)
            for nm in ("gkh", "gv"):
                m["b_" + nm] = np.concatenate([r0[nm][:, 256 * h:256 * h + 256], r1[nm][:, 256 * h:256 * h + 256]], axis=0)
            m["b_geb"] = np.concatenate([r0["geb"][:, 2 * h:2 * h + 2], r1["geb"][:, 2 * h:2 * h + 2]], axis=2)
            maps.append({k: np.ascontiguousarray(v) for k, v in m.items()})
        resB = run_bass_kernel_spmd(nc, maps, core_ids=cores).results
        last = (l == depth - 1)
        nc, _ = build_C(l, cfg, last)
        maps = []
        for c in cores:
            b, h = c // 2, c % 2
            y0, y1 = resB[2 * b]["b_y"], resB[2 * b + 1]["b_y"]
            ts = slice(h * TOK, (h + 1) * TOK)
            yT = np.concatenate([y0[0:256, ts], y1[0:256, ts], y0[256:768, ts], y1[256:768, ts], y0[768:1024, ts], y1[768:1024, ts]], axis=0)
            m = {n: v for n, v in host_consts(inp, l, h).items() if n in CONST_C}
            m.update(mod_inputs(inp, l, b))
            m["w_in"] = inp["w_in"][l]
            m["w_br_pool"] = inp["w_br_pool"][l]
            m["w_br_sb"] = inp["w_br_sb"][l]
            m["w_br_gla"] = inp["w_br_gla"][l]
            m["w_out"] = inp["w_out"][l]
            m["w_ff_gate"] = inp["w_ff_gate"][l]
            m["w_ff_up"] = inp["w_ff_up"][l]
            m["w_ff_down"] = inp["w_ff_down"][l]
            m["xT"] = xT[c]
            m["yT"] = np.ascontiguousarray(yT)
            maps.append(m)
        resC = run_bass_kernel_spmd(nc, maps, core_ids=cores).results
        if last:
            for c in cores:
                b, h = c // 2, c % 2
                out[b, h * TOK:(h + 1) * TOK] = resC[c]["out"]
        else:
            xT = [resC[c]["xT_out"] for c in cores]
        run_module.dbg = dict(A=res, B=resB, C=resC)
    return out


def kernel(**inputs):
    return run_module(inputs, 4, 8192)
```

```python
import numpy as np
import ml_dtypes
from contextlib import ExitStack
import concourse.bass as bass
import concourse.mybir as mybir
from concourse.bass_utils import run_bass_kernel_spmd

F32 = mybir.dt.float32
BF16 = mybir.dt.bfloat16
AF = mybir.ActivationFunctionType
ALU = mybir.AluOpType
NPBF = ml_dtypes.bfloat16

D = 2048
KD = 16
DFF = 5632
KF = 44
INC = 11792
TT = 512
EPS = 1e-6
NEG = -30000.0

C_U, C_SQ, C_SK, C_SV, C_GQ, C_GK, C_GV, C_GR, C_GA, C_GATE = 0, 512, 1536, 2560, 3584, 4096, 4608, 5120, 5632, 5648


class Buf:
    __slots__ = ("name", "w", "r")

    def __init__(self, name=""):
        self.name = name
        self.w = []
        self.r = {}


class KB:
    SELF_WAIT = ("act", "dve", "pool")

    def __init__(self, nc, es, n_dma_sems=8):
        self.nc = nc
        self.es = es
        self.engs = {"pe": nc.tensor, "act": nc.scalar, "dve": nc.vector, "pool": nc.gpsimd, "sp": nc.sync}
        self.semh = {}
        self.cnt = {}
        self.seen = {e: {} for e in self.engs}
        for e in self.engs:
            self.semh[e] = es.enter_context(nc.semaphore("sem_" + e))
            self.cnt[e] = 0
        self.dma_pool = {}
        for q in ("sp", "pool", "act"):
            keys = []
            for j in range(n_dma_sems):
                k = "dq_%s_%d" % (q, j)
                self.semh[k] = es.enter_context(nc.semaphore(k))
                self.cnt[k] = 0
                keys.append(k)
            self.dma_pool[q] = [keys, 0]
        self.n_inst = 0

    def _wait(self, eng, tok):
        key, val = tok
        if key == eng and eng not in self.SELF_WAIT:
            return
        if self.seen[eng].get(key, 0) >= val:
            return
        self.engs[eng].wait_ge(self.semh[key], val)
        self.seen[eng][key] = val

    def _deps(self, eng, reads, writes):
        for b in reads:
            for t in b.w:
                self._wait(eng, t)
        for b in writes:
            for t in b.w:
                self._wait(eng, t)
            for k, v in b.r.items():
                self._wait(eng, (k, v))

    def _commit(self, tok, reads, writes):
        for b in reads:
            if b.r.get(tok[0], 0) < tok[1]:
                b.r[tok[0]] = tok[1]
        for b in writes:
            b.w = [tok]
            b.r = {}

    def op(self, eng, fn, reads=(), writes=()):
        self._deps(eng, reads, writes)
        ins = fn(self.engs[eng])
        self.cnt[eng] += 1
        ins.then_inc(self.semh[eng], 1)
        tok = (eng, self.cnt[eng])
        self._commit(tok, reads, writes)
        self.n_inst += 1
        return tok

    def dma(self, q, out, in_, reads=(), writes=(), add_write=False):
        if add_write:
            for b in reads:
                for t in b.w:
                    self._wait(q, t)
            for b in writes:
                if b.w:
                    self._wait(q, b.w[0])
                for k, v in b.r.items():
                    self._wait(q, (k, v))
        else:
            self._deps(q, reads, writes)
        keys, idx = self.dma_pool[q]
        k = keys[idx % len(keys)]
        self.dma_pool[q][1] = idx + 1
        if self.cnt[k] > 0:
            self._wait(q, (k, self.cnt[k]))
        ins = self.engs[q].dma_start(out=out, in_=in_)
        self.cnt[k] += 16
        ins.then_inc(self.semh[k], 16)
        tok = (k, self.cnt[k])
        for b in reads:
            if b.r.get(tok[0], 0) < tok[1]:
                b.r[tok[0]] = tok[1]
        for b in writes:
            if add_write:
                b.w = b.w + [tok]
            else:
                b.w = [tok]
                b.r = {}
        self.n_inst += 1
        return tok

    def collective(self, in_ap, out_ap, groups, reads=(), writes=(), add_write=False, kind="AllGather", semkey=None):
        if semkey is None:
            self.n_cc = getattr(self, "n_cc", 0) + 1
            semkey = "cc%d" % self.n_cc
        key = semkey
        if key not in self.semh:
            self.semh[key] = self.es.enter_context(self.nc.semaphore("sem_" + key))
            self.cnt[key] = 0
        self._deps("pool", reads, writes)
        ins = self.nc.gpsimd.collective_compute(kind, ALU.bypass if kind == "AllGather" else ALU.add, replica_groups=groups,
                                                ins=[in_ap], outs=[out_ap])
        self.cnt[key] += 1
        ins.then_inc(self.semh[key])
        tok = (key, self.cnt[key])
        if add_write:
            prev = [list(b.w) for b in writes]
            self._commit(tok, reads, writes)
            for b, p in zip(writes, prev):
                b.w = p + [tok]
        else:
            self._commit(tok, reads, writes)
        return tok

    def barrier(self):
        for e in self.engs:
            for k, v in self.cnt.items():
                if v > 0 and k != e:
                    self._wait(e, (k, v))

    def finish(self, bufs):
        for b in bufs:
            for t in b.w:
                self._wait("sp", t)
        self.barrier()


class Ctx:
    pass


_UNIQ = [0]


def sb(kb, es, name, shape, dt):
    _UNIQ[0] += 1
    return es.enter_context(kb.nc.sbuf_tensor("%s_u%d" % (name, _UNIQ[0]), list(shape), dt))


def ps(kb, es, name, shape, dt):
    _UNIQ[0] += 1
    return es.enter_context(kb.nc.psum_tensor("%s_u%d" % (name, _UNIQ[0]), list(shape), dt))


CONST_SHARED = {"ident": ([128, 128], BF16), "ones": ([128, 128], BF16), "identf": ([128, 128], F32), "jrev": ([128, 128], BF16),
                "reset": ([128, TT], F32), "sbmask": ([128, 128], BF16), "glamask": ([128, 128], F32), "zeros": ([128, TT + 1], F32),
                "pcoef": ([128, 2, 4], F32), "pinv": ([128, 2, 16], F32), "hsel": ([128, 2], F32)}
CONST_LAYER = {"gq": ([128, 1], F32), "gk": ([128, 1], F32), "wa2": ([16, 512], F32), "ba2": ([128, 4], F32), "gog": ([128, 1], F32),
               "wpool": ([128, 2, 128], F32), "pscale": ([128, 2], F32)}


def emit_consts(kb, es, dr, names, suffix=""):
    C = {}
    for name, (shape, dt) in names.items():
        t = sb(kb, es, "c_" + name + suffix, shape, dt)
        b_ = Buf(name)
        kb.dma("sp", t[tuple(slice(None) for _ in shape)], dr[name + suffix], writes=[b_])
        C[name] = (t, b_)
    return C


def emit_mod(kb, es, dr, l, C):
    nc = kb.nc
    ccol = sb(kb, es, "ccol%d" % l, [128, KD], F32)
    scb = sb(kb, es, "scb%d" % l, [128, KD], BF16)
    bada = sb(kb, es, "bada%d" % l, [128, 96], F32)
    g1 = sb(kb, es, "g1_%d" % l, [128, KD], F32)
    g2 = sb(kb, es, "g2_%d" % l, [128, KD], F32)
    mod = sb(kb, es, "mod%d" % l, [128, 96], F32)
    gs = sb(kb, es, "gs%d" % l, [128, 2 * KD], F32)
    bmod = Buf("mod")
    bc = Buf("c")
    kb.dma("sp", ccol[:, :], dr["c_col"], writes=[bc])
    kb.dma("sp", bada[:, :], dr["b_ada%d" % l], writes=[bc], add_write=True)
    kb.dma("sp", g1[:, :], dr["g_norm1_%d" % l], writes=[bc], add_write=True)
    kb.dma("sp", g2[:, :], dr["g_norm2_%d" % l], writes=[bc], add_write=True)
    bsc = Buf("sc")
    kb.op("act", lambda e: e.activation(out=scb[:, :], in_=ccol[:, :], func=AF.Silu), reads=[bc], writes=[bsc])
    with ExitStack() as es2:
        slabs = [sb(kb, es2, "adaslab%d_%d" % (l, i), [128, KD, 512], BF16) for i in range(2)]
        bsl = [Buf("adaslab%d" % i) for i in range(2)]
        mps = ps(kb, es2, "modps%d" % l, [128, 96], F32)
        bps = Buf("modps")
        wada = dr["w_ada%d" % l]
        for s in range(24):
            sl, bs = slabs[s % 2], bsl[s % 2]
            kb.dma("pool", sl[:, :, :], wada[:, 512 * s:512 * s + 512].rearrange("(kc p) c -> p kc c", p=128),
                   writes=[bs])

            def f(e, s=s, sl=sl):
                ins = None
                for j in range(4):
                    col = 4 * s + j
                    for k in range(KD):
                        ins = e.matmul(mps[:, col:col + 1], lhsT=sl[:, k, 128 * j:128 * j + 128], rhs=scb[:, k:k + 1],
                                       start=(k == 0), stop=(k == KD - 1))
                return ins
            kb.op("pe", f, reads=[bs, bsc], writes=[bps])
        kb.op("dve", lambda e: e.tensor_tensor(out=mod[:, :], in0=mps[:, :], in1=bada[:, :], op=ALU.add),
              reads=[bps, bc], writes=[bmod])
        kb.barrier()
    kb.op("dve", lambda e: e.scalar_tensor_tensor(out=gs[:, 0:16], in0=mod[:, 16:32], scalar=1.0, in1=g1[:, :],
                                                  op0=ALU.add, op1=ALU.mult), reads=[bmod, bc], writes=[bmod])
    kb.op("dve", lambda e: e.scalar_tensor_tensor(out=gs[:, 16:32], in0=mod[:, 64:80], scalar=1.0, in1=g2[:, :],
                                                  op0=ALU.add, op1=ALU.mult), reads=[bmod, bc], writes=[bmod])
    M = Ctx()
    M.buf = bmod
    M.sh1 = mod[:, 0:16]
    M.gs1 = gs[:, 0:16]
    M.ga1 = mod[:, 32:48]
    M.sh2 = mod[:, 48:64]
    M.gs2 = gs[:, 16:32]
    M.ga2 = mod[:, 80:96]
    return M


def emit_wcast(kb, dr, src, dst, rows, c0, c1, rchunk=256, r_dst0=0, first=True):
    buf = dr["_b_" + dst]
    for r0 in range(0, rows, rchunk):
        r1 = min(rows, r0 + rchunk)
        kb.dma("pool", dr[dst][r_dst0 + r0:r_dst0 + r1, c0:c1], dr[src][r0:r1, c0:c1], writes=[buf], add_write=not first)
        first = False


def emit_norm(kb, N, xT, bx, hT, bh, gs, sh, M, C):
    ones, bones = C["ones"]
    for k in range(KD):
        sq, bsq = N.sq[k % 2], N.bsq[k % 2]
        kb.op("act", lambda e, k=k, sq=sq: e.activation(out=sq[:, :], in_=xT[:, k, :], func=AF.Square),
              reads=[bx[k]], writes=[bsq])
        kb.op("pe", lambda e, k=k, sq=sq: e.matmul(N.ssps[:, :], lhsT=ones[:, :], rhs=sq[:, :], start=(k == 0),
                                                    stop=(k == KD - 1)), reads=[bsq, bones], writes=[N.bssps])
    kb.op("act", lambda e: e.activation(out=N.rstd[:, :], in_=N.ssps[:, :], func=AF.Sqrt, scale=1.0 / D, bias=N.epsc[:, 0:1]),
          reads=[N.bssps, N.beps], writes=[N.brstd])
    kb.op("dve", lambda e: e.reciprocal(out=N.rstd[:, :], in_=N.rstd[:, :]), reads=[N.brstd], writes=[N.brstd])
    for k in range(KD):
        tmp, btmp = N.tmp[k % 2], N.btmp[k % 2]
        kb.op("dve", lambda e, k=k, tmp=tmp: e.scalar_tensor_tensor(out=tmp[:, :], in0=xT[:, k, :], scalar=gs[:, k:k + 1],
                                                                    in1=N.rstd[:, :], op0=ALU.mult, op1=ALU.mult),
              reads=[bx[k], N.brstd, M.buf], writes=[btmp])
        kb.op("act", lambda e, k=k, tmp=tmp: e.activation(out=hT[:, k, :], in_=tmp[:, :], func=AF.Identity,
                                                          bias=sh[:, k:k + 1], scale=1.0),
              reads=[btmp, M.buf], writes=[bh[k]])


def alloc_norm(kb, es):
    N = Ctx()
    N.sq = [sb(kb, es, "nsq%d" % i, [128, TT], BF16) for i in range(2)]
    N.bsq = [Buf() for _ in range(2)]
    N.tmp = [sb(kb, es, "ntmp%d" % i, [128, TT], F32) for i in range(2)]
    N.btmp = [Buf() for _ in range(2)]
    N.rstd = sb(kb, es, "nrstd", [128, TT], F32)
    N.brstd = Buf()
    N.ssps = ps(kb, es, "nssps", [128, TT], F32)
    N.bssps = Buf()
    N.epsc = sb(kb, es, "nepsc", [128, 1], F32)
    N.beps = Buf()
    kb.op("dve", lambda e: e.memset(N.epsc[:, :], EPS), writes=[N.beps])
    return N


def emit_phaseC(kb, dr, l, cfg, C, M, last, fx):
    nc = kb.nc
    TOK = cfg["TOK"]
    NT = TOK // TT
    with ExitStack() as es:
        xT = sb(kb, es, "c_xT", [128, KD, TT], F32)
        bx = [Buf("xT%d" % k) for k in range(KD)]
        hT = sb(kb, es, "c_hT", [128, KD, TT], BF16)
        bh = [Buf() for _ in range(KD)]
        yT = sb(kb, es, "c_yT", [128, KD, TT], BF16)
        by = Buf("yT")
        mT = sb(kb, es, "c_mT", [128, KD, TT], BF16)
        bm = [Buf() for _ in range(KD)]
        aT = sb(kb, es, "c_aT", [128, KF, TT], BF16)
        ba = [Buf() for _ in range(KF)]
        NS = 3
        slab = [sb(kb, es, "c_slab%d" % i, [128, KF, 128], BF16) for i in range(NS)]
        bslab = [Buf() for _ in range(NS)]
        sig = [sb(kb, es, "c_sig%d" % i, [128, TT], F32) for i in range(3)]
        bsig = [Buf() for _ in range(3)]
        macc = [sb(kb, es, "c_macc%d" % i, [128, TT], F32) for i in range(2)]
        bmacc = [Buf() for _ in range(2)]
        sg = [sb(kb, es, "c_sg%d" % i, [128, TT], F32) for i in range(2)]
        bsg = [Buf() for _ in range(2)]
        N = alloc_norm(kb, es)
        NPS = 6
        pp = [ps(kb, es, "c_ps%d" % i, [128, TT], F32) for i in range(NPS)]
        bpp = [Buf() for _ in range(NPS)]
        if last:
            ot = [sb(kb, es, "c_ot%d" % i, [128, D], F32) for i in range(1)]
            bot = [Buf() for _ in range(1)]
            identf, bidf = C["identf"]
        st = {"slab": 0, "ps": 0}

        def next_slab():
            i = st["slab"] % NS
            st["slab"] += 1
            return slab[i], bslab[i]

        def next_ps():
            i = st["ps"] % NPS
            st["ps"] += 1
            return pp[i], bpp[i]

        def load_slab(wname, nk, c0, ncols=128):
            sl, bs = next_slab()
            kb.dma("sp", sl[:, 0:nk, 0:ncols], dr[wname][:, c0:c0 + ncols].rearrange("(kc p) c -> p kc c", p=128),
                   reads=[dr["_b_" + wname]], writes=[bs])
            return sl, bs

        def mm_group(pt, bpt, sl, bs, nk, rhs, brhs):
            def f(e):
                ins = None
                for k in range(nk):
                    ins = e.matmul(pt[:, :], lhsT=sl[:, k, 0:128], rhs=rhs[:, k, :], start=(k == 0), stop=(k == nk - 1))
                return ins
            kb.op("pe", f, reads=[bs] + list(brhs), writes=[bpt])

        for ti in range(NT):
            t0 = ti * TT
            kb.dma("sp", xT[:, :, :], dr["xT"][:, t0:t0 + TT].rearrange("(kc p) t -> p kc t", p=128),
                   reads=[dr["_b_xT"]], writes=bx)
            fw = True
            for (k0, nk, yr0) in ((0, 2, 0), (4, 4, 256), (12, 2, 768)):
                for hh in range(2):
                    kb.dma("sp", yT[:, k0 + hh * nk:k0 + (hh + 1) * nk, :], fx.c_y(hh, yr0, 128 * nk, t0).rearrange("(kc p) t -> p kc t", p=128),
                           reads=[fx.bWr], writes=[by], add_write=not fw)
                    fw = False
            emit_norm(kb, N, xT, bx, hT, bh, M.gs1, M.sh1, M, C)
            for fc in range(KD):
                gate_ps = []
                for i in range(3):
                    sl, bs = load_slab("w_in_bf", KD, C_GATE + i * D + fc * 128)
                    pt, bpt = next_ps()
                    mm_group(pt, bpt, sl, bs, KD, hT, bh)
                    kb.op("act", lambda e, i=i, pt=pt: e.activation(out=sig[i][:, :], in_=pt[:, :], func=AF.Sigmoid),
                          reads=[bpt], writes=[bsig[i]])
                sl, bs = load_slab("w_br_bf", KD, fc * 128)
                mi = fc % 2
                for i, (k0, k1) in enumerate(((0, 4), (4, 12), (12, 16))):
                    pt, bpt = next_ps()

                    def f(e, pt=pt, sl=sl, k0=k0, k1=k1):
                        ins = None
                        for k in range(k0, k1):
                            ins = e.matmul(pt[:, :], lhsT=sl[:, k, 0:128], rhs=yT[:, k, :], start=(k == k0), stop=(k == k1 - 1))
                        return ins
                    kb.op("pe", f, reads=[bs, by], writes=[bpt])
                    if i == 0:
                        kb.op("dve", lambda e, pt=pt: e.tensor_tensor(out=macc[mi][:, :], in0=pt[:, :], in1=sig[0][:, :], op=ALU.mult),
                              reads=[bpt, bsig[0]], writes=[bmacc[mi]])
                    else:
                        sgi = i - 1
                        kb.op("dve", lambda e, pt=pt, i=i, sgi=sgi: e.tensor_tensor(out=sg[sgi][:, :], in0=pt[:, :], in1=sig[i][:, :], op=ALU.mult),
                              reads=[bpt, bsig[i]], writes=[bsg[sgi]])
                        if i == 1:
                            kb.op("pool", lambda e, sgi=sgi: e.tensor_tensor(out=macc[mi][:, :], in0=macc[mi][:, :], in1=sg[sgi][:, :], op=ALU.add),
                                  reads=[bsg[sgi], bmacc[mi]], writes=[bmacc[mi]])
                        else:
                            kb.op("pool", lambda e, sgi=sgi, fc=fc: e.tensor_tensor(out=mT[:, fc, :], in0=macc[mi][:, :], in1=sg[sgi][:, :], op=ALU.add),
                                  reads=[bsg[sgi], bmacc[mi]], writes=[bm[fc]])
            for fc in range(KD):
                sl, bs = load_slab("w_out_bf", KD, fc * 128)
                pt, bpt = next_ps()
                mm_group(pt, bpt, sl, bs, KD, mT, bm)
                kb.op("dve", lambda e, pt=pt, fc=fc: e.scalar_tensor_tensor(out=xT[:, fc, :], in0=pt[:, :], scalar=M.ga1[:, fc:fc + 1],
                                                                            in1=xT[:, fc, :], op0=ALU.mult, op1=ALU.add),
                      reads=[bpt, M.buf, bx[fc]], writes=[bx[fc]])
            emit_norm(kb, N, xT, bx, hT, bh, M.gs2, M.sh2, M, C)
            for j in range(KF):
                slg, bsg_ = load_slab("w_gate_bf", KD, j * 128)
                slu, bsu = load_slab("w_up_bf", KD, j * 128)
                pg, bpg = next_ps()
                pu, bpu = next_ps()
                mm_group(pg, bpg, slg, bsg_, KD, hT, bh)
                mm_group(pu, bpu, slu, bsu, KD, hT, bh)
                si = j % 2
                kb.op("act", lambda e, pg=pg, si=si: e.activation(out=sg[si][:, :], in_=pg[:, :], func=AF.Silu),
                      reads=[bpg], writes=[bsg[si]])
                kb.op("dve", lambda e, pu=pu, si=si, j=j: e.tensor_tensor(out=aT[:, j, :], in0=pu[:, :], in1=sg[si][:, :], op=ALU.mult),
                      reads=[bpu, bsg[si]], writes=[ba[j]])
            for fc in range(KD):
                sl, bs = load_slab("w_down_bf", KF, fc * 128)
                pt, bpt = next_ps()
                mm_group(pt, bpt, sl, bs, KF, aT, ba)
                kb.op("dve", lambda e, pt=pt, fc=fc: e.scalar_tensor_tensor(out=xT[:, fc, :], in0=pt[:, :], scalar=M.ga2[:, fc:fc + 1],
                                                                            in1=xT[:, fc, :], op0=ALU.mult, op1=ALU.add),
                      reads=[bpt, M.buf, bx[fc]], writes=[bx[fc]])
            if not last:
                kb.dma("pool", dr["xT_out"][:, t0:t0 + TT].rearrange("(kc p) t -> p kc t", p=128), xT[:, :, :],
                       reads=bx, writes=[dr["_b_xT_out"]], add_write=True)
            else:
                for b in range(4):
                    o, bo = ot[0], bot[0]
                    for kq in range(4):
                        pt, bpt = next_ps()

                        def f(e, pt=pt, kq=kq, b=b):
                            ins = None
                            for kk in range(4):
                                k = 4 * kq + kk
                                ins = e.transpose(pt[:, 128 * kk:128 * kk + 128], xT[:, k, 128 * b:128 * b + 128], identf[:, :])
                            return ins
                        kb.op("pe", f, reads=bx + [bidf], writes=[bpt])
                        eng = "act" if kq % 2 == 0 else "dve"
                        if eng == "act":
                            kb.op("act", lambda e, pt=pt, kq=kq, o=o: e.activation(out=o[:, 512 * kq:512 * kq + 512], in_=pt[:, :], func=AF.Copy),
                                  reads=[bpt], writes=[bo])
                        else:
                            kb.op("dve", lambda e, pt=pt, kq=kq, o=o: e.tensor_copy(out=o[:, 512 * kq:512 * kq + 512], in_=pt[:, :]),
                                  reads=[bpt], writes=[bo])
                    kb.dma("pool", dr["out"][t0 + 128 * b:t0 + 128 * b + 128, :], o[:, :], reads=[bo],
                           writes=[dr["_b_out"]], add_write=True)
        kb.barrier()


def emit_phaseA(kb, dr, l, cfg, C, M, first, fx):
    TOK = cfg["TOK"]
    NT = TOK // TT
    NCH = TOK // 128
    ident, bid = C["ident"]
    jrev, bjr = C["jrev"]
    with ExitStack() as es:
        xT = sb(kb, es, "a_xT", [128, KD, TT], F32)
        bx = [Buf() for _ in range(KD)]
        hT = sb(kb, es, "a_hT", [128, KD, TT], BF16)
        bh = [Buf() for _ in range(KD)]
        if first:
            xin = [sb(kb, es, "a_xin%d" % i, [128, D], F32) for i in range(4)]
            bxin = [Buf() for _ in range(4)]
            identf, bidf = C["identf"]
        slab = [sb(kb, es, "a_slab%d" % i, [128, KD, 512], BF16) for i in range(2)]
        bslab = [Buf() for _ in range(2)]
        N = alloc_norm(kb, es)
        NPS = 5
        pp = [ps(kb, es, "a_ps%d" % i, [128, TT], F32) for i in range(NPS)]
        bpp = [Buf() for _ in range(NPS)]
        NSTG = 6
        stg = [sb(kb, es, "a_stg%d" % i, [128, TT], BF16) for i in range(NSTG)]
        bstg = [Buf() for _ in range(NSTG)]
        qn = [sb(kb, es, "a_qn%d" % i, [128, TT], BF16) for i in range(4)]
        bqn = [Buf() for _ in range(4)]
        junk = sb(kb, es, "a_junk", [128, 128], BF16)
        bjunk = Buf()
        ss = sb(kb, es, "a_ss", [128, 16], F32)
        bss = Buf()
        rs = sb(kb, es, "a_rs", [128, 16], F32)
        brs = Buf()
        wa2b = sb(kb, es, "a_wa2b", [16, 512], BF16)
        nba2 = sb(kb, es, "a_nba2", [128, 4], F32)
        gqs = sb(kb, es, "a_gqs", [128, 1], F32)
        bset = Buf()
        alow = sb(kb, es, "a_alow", [16, TT], BF16)
        balow = Buf()
        g1t = sb(kb, es, "a_g1t", [128, TT], F32)
        g2t = sb(kb, es, "a_g2t", [128, TT], F32)
        cum = sb(kb, es, "a_cum", [128, TT], F32)
        bg1, bg2, bcum = Buf(), Buf(), Buf()
        nb = sb(kb, es, "a_nb", [128, 4], F32)
        bnb = Buf()
        EB = [[sb(kb, es, "a_eb%d_%d" % (h, i), [128, TT], F32) for i in range(3)] for h in range(4)]
        bEB = [[Buf() for i in range(3)] for h in range(4)]
        ebt = sb(kb, es, "a_ebt", [128, 4, NCH], F32)
        bebt = Buf()
        khT = sb(kb, es, "a_khT", [128, TT], BF16)
        bkhT = Buf()
        vst = [sb(kb, es, "a_vst%d" % i, [128, TT], BF16) for i in range(2)]
        bvst = [Buf() for _ in range(2)]
        st = {"slab": 0, "ps": 0, "stg": 0, "ev": 0}

        wa2, bwa2 = C["wa2"]
        ba2, bba2 = C["ba2"]
        gq, bgq = C["gq"]
        gk, bgk = C["gk"]
        reset, breset = C["reset"]
        kb.op("dve", lambda e: e.tensor_copy(out=wa2b[:, :], in_=wa2[:, :]), reads=[bwa2], writes=[bset])
        kb.op("dve", lambda e: e.tensor_scalar(out=nba2[:, :], in0=ba2[:, :], scalar1=-1.0, scalar2=None, op0=ALU.mult),
              reads=[bba2], writes=[bset])
        kb.op("dve", lambda e: e.tensor_scalar(out=gqs[:, :], in0=gq[:, :], scalar1=128.0 ** -0.5, scalar2=None, op0=ALU.mult),
              reads=[bgq], writes=[bset])

        def next_ps():
            i = st["ps"] % NPS
            st["ps"] += 1
            return pp[i], bpp[i]

        def next_stg():
            i = st["stg"] % NSTG
            st["stg"] += 1
            return stg[i], bstg[i]

        def load_slab(c0, ncols):
            i = st["slab"] % 2
            st["slab"] += 1
            sl, bs = slab[i], bslab[i]
            kb.dma("sp", sl[:, :, 0:ncols], dr["w_in_bf"][:, c0:c0 + ncols].rearrange("(kc p) c -> p kc c", p=128),
                   reads=[dr["_b_w_in_bf"]], writes=[bs])
            return sl, bs

        def fm_chunk(sl, bs, j, m=128):
            pt, bpt = next_ps()

            def f(e):
                ins = None
                for k in range(KD):
                    ins = e.matmul(pt[0:m, :], lhsT=sl[:, k, 128 * j:128 * j + m], rhs=hT[:, k, :], start=(k == 0), stop=(k == KD - 1))
                return ins
            kb.op("pe", f, reads=[bs] + bh, writes=[bpt])
            return pt, bpt

        def tm_block(sl, bs, b):
            pt, bpt = next_ps()

            def f(e):
                ins = None
                for k in range(KD):
                    ins = e.matmul(pt[:, :], lhsT=hT[:, k, 128 * b:128 * b + 128], rhs=sl[:, k, :], start=(k == 0), stop=(k == KD - 1))
                return ins
            kb.op("pe", f, reads=[bs] + bh, writes=[bpt])
            return pt, bpt

        def evac(out_ap, in_ap, reads, writes, scale=None, sreads=()):
            st["ev"] += 1
            if st["ev"] % 2 == 0:
                if scale is None:
                    kb.op("act", lambda e: e.activation(out=out_ap, in_=in_ap, func=AF.Copy), reads=reads, writes=writes)
                else:
                    kb.op("act", lambda e: e.activation(out=out_ap, in_=in_ap, func=AF.Copy, scale=scale),
                          reads=list(reads) + list(sreads), writes=writes)
            else:
                if scale is None:
                    kb.op("dve", lambda e: e.tensor_copy(out=out_ap, in_=in_ap), reads=reads, writes=writes)
                else:
                    kb.op("dve", lambda e: e.tensor_scalar(out=out_ap, in0=in_ap, scalar1=scale, scalar2=None, op0=ALU.mult),
                          reads=list(reads) + list(sreads), writes=writes)

        def store(dst_ap, src_ap, bsrc, bdst):
            kb.dma("pool", dst_ap, src_ap, reads=(bsrc if isinstance(bsrc, list) else [bsrc]), writes=[bdst], add_write=True)

        for ti in range(NT):
            t0 = ti * TT
            if first:
                for b in range(4):
                    kb.dma("sp", xin[b][:, :], dr["x_tok"][t0 + 128 * b:t0 + 128 * b + 128, :], writes=[bxin[b]])
                for k in range(KD):
                    pt, bpt = next_ps()

                    def f(e, pt=pt, k=k):
                        ins = None
                        for b in range(4):
                            ins = e.transpose(pt[:, 128 * b:128 * b + 128], xin[b][:, 128 * k:128 * k + 128], identf[:, :])
                        return ins
                    kb.op("pe", f, reads=bxin + [bidf], writes=[bpt])
                    evac(xT[:, k, :], pt[:, :], [bpt], [bx[k]])
                store(dr["xT_out"][:, t0:t0 + TT].rearrange("(kc p) t -> p kc t", p=128), xT[:, :, :], bx, dr["_b_xT_out"])
            else:
                kb.dma("sp", xT[:, :, :], dr["xT"][:, t0:t0 + TT].rearrange("(kc p) t -> p kc t", p=128),
                       reads=[dr["_b_xT"]], writes=bx)
            emit_norm(kb, N, xT, bx, hT, bh, M.gs1, M.sh1, M, C)

            sl, bs = load_slab(C_GA, 16)
            pt, bpt = fm_chunk(sl, bs, 0, m=16)
            kb.op("dve", lambda e, pt=pt: e.tensor_copy(out=alow[:, :], in_=pt[0:16, :]), reads=[bpt], writes=[balow])
            for hd in range(4):
                zp, bzp = next_ps()
                kb.op("pe", lambda e, zp=zp, hd=hd: e.matmul(zp[:, :], lhsT=wa2b[0:16, 128 * hd:128 * hd + 128], rhs=alow[0:16, :],
                                                             start=True, stop=True), reads=[balow, bset], writes=[bzp])
                kb.op("act", lambda e, zp=zp, hd=hd: e.activation(out=g1t[:, :], in_=zp[:, :], func=AF.Exp, scale=-1.0,
                                                                  bias=nba2[:, hd:hd + 1]), reads=[bzp, bset], writes=[bg1])
                kb.op("act", lambda e: e.activation(out=g2t[:, :], in_=g1t[:, :], func=AF.Ln, bias=1.0, scale=1.0),
                      reads=[bg1], writes=[bg2])
                kb.op("dve", lambda e: e.tensor_tensor_scan(out=cum[:, :], data0=reset[:, :], data1=g2t[:, :], initial=0.0,
                                                            op0=ALU.mult, op1=ALU.add), reads=[bg2, breset], writes=[bcum])
                kb.op("act", lambda e, hd=hd: e.activation(out=EB[hd][0][:, :], in_=cum[:, :], func=AF.Exp, scale=-1.0 / 16),
                      reads=[bcum], writes=[bEB[hd][0]])
                kb.op("act", lambda e, hd=hd: e.activation(out=EB[hd][1][:, :], in_=cum[:, :], func=AF.Exp, scale=1.0 / 16),
                      reads=[bcum], writes=[bEB[hd][1]])
                kb.op("dve", lambda e: e.tensor_scalar(out=nb[:, :], in0=cum[:, :].rearrange("p (c t) -> p c t", t=128)[:, :, 127],
                                                       scalar1=-1.0 / 16, scalar2=None, op0=ALU.mult), reads=[bcum], writes=[bnb])
                for c in range(4):
                    kb.op("act", lambda e, hd=hd, c=c: e.activation(out=EB[hd][2][:, 128 * c:128 * c + 128], in_=cum[:, 128 * c:128 * c + 128],
                                                                    func=AF.Exp, scale=1.0 / 16, bias=nb[:, c:c + 1]),
                          reads=[bcum, bnb], writes=[bEB[hd][2]])
                kb.op("act", lambda e, hd=hd: e.activation(out=ebt[:, hd, 4 * ti:4 * ti + 4], in_=nb[:, :], func=AF.Exp),
                      reads=[bnb], writes=[bebt])
            sl, bs = load_slab(C_GQ, 512)
            for hd in range(4):
                pt, bpt = fm_chunk(sl, bs, hd)
                sg_, bsg_ = next_stg()
                kb.op("dve", lambda e, pt=pt, hd=hd, sg_=sg_: e.scalar_tensor_tensor(out=sg_[:, :], in0=pt[:, :], scalar=128.0 ** -0.5,
                                                                                   in1=EB[hd][0][:, :], op0=ALU.mult, op1=ALU.mult),
                      reads=[bpt, bEB[hd][0]], writes=[bsg_])
                store(fx.a_fm("gqT", 128 * hd, 128, t0, TT), sg_[:, :], bsg_, fx.bZFw)
            sl, bs = load_slab(C_GK, 512)
            for hd in range(4):
                pt, bpt = fm_chunk(sl, bs, hd)
                sg_, bsg_ = next_stg()
                kb.op("dve", lambda e, pt=pt, hd=hd, sg_=sg_: e.tensor_tensor(out=sg_[:, :], in0=pt[:, :], in1=EB[hd][1][:, :], op=ALU.mult),
                      reads=[bpt, bEB[hd][1]], writes=[bsg_])
                store(fx.a_fm("gkT", 128 * hd, 128, t0, TT), sg_[:, :], bsg_, fx.bZFw)
                kb.op("dve", lambda e, pt=pt, hd=hd: e.tensor_tensor(out=khT[:, :], in0=pt[:, :], in1=EB[hd][2][:, :], op=ALU.mult),
                      reads=[bpt, bEB[hd][2]], writes=[bkhT])
                p2, bp2 = next_ps()

                def f(e, p2=p2):
                    ins = None
                    for b in range(4):
                        ins = e.matmul(p2[:, 128 * b:128 * b + 128], lhsT=khT[:, 128 * b:128 * b + 128], rhs=ident[:, :], start=True, stop=True)
                    return ins
                kb.op("pe", f, reads=[bkhT, bid], writes=[bp2])
                sg2, bsg2 = next_stg()
                evac(sg2[:, :], p2[:, :], [bp2], [bsg2])
                for b4 in range(4):
                    store(fx.a_tm("gkh", t0 + 128 * b4, 128, 128 * hd, 128), sg2[:, 128 * b4:128 * b4 + 128], bsg2, fx.bZTw)
            sl, bs = load_slab(C_GR, 512)
            for hd in range(4):
                pt, bpt = fm_chunk(sl, bs, hd)
                sg_, bsg_ = next_stg()
                kb.op("act", lambda e, pt=pt, sg_=sg_: e.activation(out=sg_[:, :], in_=pt[:, :], func=AF.Silu), reads=[bpt], writes=[bsg_])
                store(fx.a_fm("grT", 128 * hd, 128, t0, TT), sg_[:, :], bsg_, fx.bZFw)
            sl, bs = load_slab(C_GV, 512)
            for b in range(4):
                pt, bpt = tm_block(sl, bs, b)
                sg_, bsg_ = next_stg()
                evac(sg_[:, :], pt[:, :], [bpt], [bsg_])
                store(fx.a_tm("gv", t0 + 128 * b, 128, 0, 256), sg_[:, 0:256], bsg_, fx.bZTw)
                store(fx.a_tm("gv", t0 + 128 * b, 128, 256, 256), sg_[:, 256:512], bsg_, fx.bZTw)
            sl, bs = load_slab(C_U, 512)
            for j in range(4):
                pt, bpt = fm_chunk(sl, bs, j)
                sg_, bsg_ = next_stg()
                evac(sg_[:, :], pt[:, :], [bpt], [bsg_])
                store(fx.a_fm("uT", 128 * j, 128, t0, TT), sg_[:, :], bsg_, fx.bZFw)
            for which, c0, gain, bgain, perm, bperm, dname in (("q", C_SQ, gqs, bset, ident, bid, "sbqT"),
                                                               ("k", C_SK, gk, bgk, jrev, bjr, "sbkT")):
                for half in range(2):
                    sl, bs = load_slab(c0 + 512 * half, 512)
                    for b in range(4):
                        pt, bpt = tm_block(sl, bs, b)
                        for hh in range(4):
                            kb.op("act", lambda e, pt=pt, hh=hh, b=b: e.activation(out=junk[:, :], in_=pt[:, 128 * hh:128 * hh + 128], func=AF.Square,
                                                                                    accum_out=ss[:, 4 * b + hh:4 * b + hh + 1]),
                                  reads=[bpt], writes=[bjunk, bss])
                        kb.op("act", lambda e, b=b: e.activation(out=rs[:, 4 * b:4 * b + 4], in_=ss[:, 4 * b:4 * b + 4], func=AF.Sqrt, scale=1.0 / 128,
                                                                 bias=N.epsc[:, 0:1]), reads=[bss, N.beps], writes=[brs])
                        kb.op("dve", lambda e, b=b: e.reciprocal(out=rs[:, 4 * b:4 * b + 4], in_=rs[:, 4 * b:4 * b + 4]), reads=[brs], writes=[brs])
                        for hh in range(4):
                            evac(qn[b][:, 128 * hh:128 * hh + 128], pt[:, 128 * hh:128 * hh + 128], [bpt], [bqn[b]],
                                 scale=rs[:, 4 * b + hh:4 * b + hh + 1], sreads=[brs])
                    for hh in range(4):
                        p2, bp2 = next_ps()

                        def f(e, p2=p2, hh=hh, perm=perm, which=which):
                            ins = None
                            for b in range(4):
                                bb = b if which == "q" else 3 - b
                                ins = e.matmul(p2[:, 128 * bb:128 * bb + 128], lhsT=qn[b][:, 128 * hh:128 * hh + 128], rhs=perm[:, :],
                                               start=True, stop=True)
                            return ins
                        kb.op("pe", f, reads=bqn + [bperm], writes=[bp2])
                        sg_, bsg_ = next_stg()
                        evac(sg_[:, :], p2[:, :], [bp2], [bsg_], scale=gain[:, 0:1], sreads=[bgain])
                        head = 4 * half + hh
                        if which == "q":
                            store(fx.a_fm(dname, 128 * head, 128, t0, TT), sg_[:, :], bsg_, fx.bZFw)
                        else:
                            store(fx.a_fm(dname, 128 * head, 128, TOK - t0 - TT, TT), sg_[:, :], bsg_, fx.bZFw)
            for half in range(2):
                sl, bs = load_slab(C_SV + 512 * half, 512)
                for b in range(4):
                    pt, bpt = tm_block(sl, bs, b)
                    v_, bv_ = vst[b % 2], bvst[b % 2]
                    evac(v_[:, :], pt[:, :], [bpt], [bv_])
                    p2, bp2 = next_ps()
                    kb.op("pe", lambda e, p2=p2, v_=v_: e.matmul(p2[:, :], lhsT=jrev[:, :], rhs=v_[:, :], start=True, stop=True),
                          reads=[bv_, bjr], writes=[bp2])
                    sg_, bsg_ = next_stg()
                    evac(sg_[:, :], p2[:, :], [bp2], [bsg_])
                    r0 = TOK - (t0 + 128 * b) - 128
                    store(fx.a_tm("sbv", r0, 128, 512 * half, 512), sg_[:, :], bsg_, fx.bZTw)
        store(dr["ZE_A"].rearrange("p (h c) -> p h c", h=4), ebt[:, :, :], bebt, fx.bZEw)
        kb.barrier()


def emit_phaseB_pool(kb, dr, cfg, C, fx):
    S = cfg["S"]
    NTL = S // TT
    W = TT + 16
    wpool, bwp = C["wpool"]
    pscale, bpsc = C["pscale"]
    pcoef, bpc = C["pcoef"]
    pinv, bpi = C["pinv"]
    with ExitStack() as es:
        wpb = sb(kb, es, "p_wpb", [128, 2, 128], BF16)
        bwpb = Buf()
        kb.op("dve", lambda e: e.tensor_copy(out=wpb[:, :, :], in_=wpool[:, :, :]), reads=[bwp], writes=[bwpb])
        ub = [sb(kb, es, "p_ub%d" % i, [128, W], BF16) for i in range(2)]
        bub = [Buf() for _ in range(2)]
        U = sb(kb, es, "p_U", [128, W], F32)
        S1 = sb(kb, es, "p_S1", [128, W], F32)
        S2 = sb(kb, es, "p_S2", [128, W], F32)
        S3 = sb(kb, es, "p_S3", [128, W], F32)
        S4 = sb(kb, es, "p_S4", [128, W], F32)
        acc = sb(kb, es, "p_acc", [128, W], F32)
        bw = Buf()
        pl = [sb(kb, es, "p_pl%d" % i, [128, TT], BF16) for i in range(2)]
        bpl = [Buf() for _ in range(2)]
        stg = [sb(kb, es, "p_stg%d" % i, [128, TT], BF16) for i in range(2)]
        bstg = [Buf() for _ in range(2)]
        pp = [ps(kb, es, "p_ps%d" % i, [128, TT], F32) for i in range(2)]
        bpp = [Buf() for _ in range(2)]
        n = 0
        for g in range(2):
            for j in range(NTL):
                u_, bu_ = ub[n % 2], bub[n % 2]
                TOK = cfg["TOK"]
                th = (TT * j) // TOK
                lc = TT * j - th * TOK
                if j == 0:
                    kb.op("dve", lambda e, u_=u_: e.memset(u_[:, 0:16], 0.0), writes=[bu_])
                    kb.dma("sp", u_[:, 16:W], fx.b_fm("uT", 128 * g, 128, 0, 0, TT), reads=[fx.bZFr], writes=[bu_], add_write=True)
                elif lc == 0:
                    kb.dma("sp", u_[:, 0:16], fx.b_fm("uT", 128 * g, 128, th - 1, TOK - 16, 16), reads=[fx.bZFr], writes=[bu_])
                    kb.dma("sp", u_[:, 16:W], fx.b_fm("uT", 128 * g, 128, th, 0, TT), reads=[fx.bZFr], writes=[bu_], add_write=True)
                else:
                    kb.dma("sp", u_[:, 0:W], fx.b_fm("uT", 128 * g, 128, th, lc - 16, TT + 16), reads=[fx.bZFr], writes=[bu_])
                kb.op("dve", lambda e, u_=u_: e.tensor_copy(out=U[:, :], in_=u_[:, :]), reads=[bu_], writes=[bw])
                kb.op("dve", lambda e: e.tensor_tensor(out=S1[:, 1:W], in0=U[:, 1:W], in1=U[:, 0:W - 1], op=ALU.add), reads=[bw], writes=[bw])
                kb.op("dve", lambda e: e.tensor_tensor(out=S2[:, 3:W], in0=S1[:, 3:W], in1=S1[:, 1:W - 2], op=ALU.add), reads=[bw], writes=[bw])
                kb.op("dve", lambda e: e.tensor_tensor(out=S3[:, 7:W], in0=S2[:, 7:W], in1=S2[:, 3:W - 4], op=ALU.add), reads=[bw], writes=[bw])
                kb.op("dve", lambda e: e.tensor_tensor(out=S4[:, 15:W], in0=S3[:, 15:W], in1=S3[:, 7:W - 8], op=ALU.add), reads=[bw], writes=[bw])
                kb.op("dve", lambda e, g=g: e.tensor_scalar(out=acc[:, 16:W], in0=S1[:, 16:W], scalar1=pcoef[:, g, 0:1], scalar2=None, op0=ALU.mult),
                      reads=[bw, bpc], writes=[bw])
                for lv, SL in ((1, S2), (2, S3), (3, S4)):
                    kb.op("dve", lambda e, g=g, lv=lv, SL=SL: e.scalar_tensor_tensor(out=acc[:, 16:W], in0=SL[:, 16:W], scalar=pcoef[:, g, lv:lv + 1],
                                                                                    in1=acc[:, 16:W], op0=ALU.mult, op1=ALU.add),
                          reads=[bw, bpc], writes=[bw])
                if j == 0:
                    kb.op("dve", lambda e, g=g: e.tensor_tensor(out=acc[:, 16:32], in0=acc[:, 16:32], in1=pinv[:, g, :], op=ALU.mult),
                          reads=[bw, bpi], writes=[bw])
                p_, bp_ = pl[n % 2], bpl[n % 2]
                kb.op("dve", lambda e, p_=p_: e.tensor_tensor(out=p_[:, :], in0=acc[:, 16:W], in1=U[:, 16:W], op=ALU.subtract),
                      reads=[bw], writes=[bp_])
                pt, bpt = pp[n % 2], bpp[n % 2]
                kb.op("pe", lambda e, pt=pt, p_=p_, g=g: e.matmul(pt[:, :], lhsT=wpb[:, g, :], rhs=p_[:, :], start=True, stop=True),
                      reads=[bp_, bwpb], writes=[bpt])
                s_, bs_ = stg[n % 2], bstg[n % 2]
                kb.op("act", lambda e, pt=pt, s_=s_, g=g: e.activation(out=s_[:, :], in_=pt[:, :], func=AF.Copy, scale=pscale[:, g:g + 1]),
                      reads=[bpt, bpsc], writes=[bs_])
                kb.dma("pool", fx.b_y(128 * g, TT * j), s_[:, :], reads=[bs_], writes=[fx.bWw], add_write=True)
                n += 1
        kb.barrier()


def emit_phaseB_gla(kb, dr, cfg, C, fx):
    S = cfg["S"]
    NB = S // 128
    ones, bones = C["ones"]
    gmask, bgm = C["glamask"]
    gog, bgog = C["gog"]
    with ExitStack() as es:
        qT = sb(kb, es, "g_qT", [128, S], BF16)
        kT = sb(kb, es, "g_kT", [128, S], BF16)
        kh = sb(kb, es, "g_kh", [128, NB, 128], BF16)
        v = sb(kb, es, "g_v", [128, NB, 128], BF16)
        rT = sb(kb, es, "g_rT", [128, S], BF16)
        bin_ = Buf()
        eb = sb(kb, es, "g_eb", [128, 2, NB], F32)
        beb = Buf()
        NCH = NB // 2
        eba = sb(kb, es, "g_eba", [128, 2, 4, NCH], F32)
        beba = Buf()
        hsel, bhsel = C["hsel"]
        kb.dma("sp", eba[:, :, :, :], dr["ZE_R"].rearrange("(t p) (h c) -> p t h c", p=128, h=4), reads=[fx.bZEr, fx.bFN], writes=[beba])
        for t in range(2):
            for j in range(2):
                kb.op("dve", lambda e, t=t, j=j: e.tensor_scalar(out=eb[:, j, t * NCH:(t + 1) * NCH], in0=eba[:, t, j, :], scalar1=hsel[:, 0:1],
                                                                 scalar2=None, op0=ALU.mult), reads=[beba, bhsel], writes=[beb])
                kb.op("dve", lambda e, t=t, j=j: e.scalar_tensor_tensor(out=eb[:, j, t * NCH:(t + 1) * NCH], in0=eba[:, t, 2 + j, :], scalar=hsel[:, 1:2],
                                                                        in1=eb[:, j, t * NCH:(t + 1) * NCH], op0=ALU.mult, op1=ALU.add),
                      reads=[beba, bhsel, beb], writes=[beb])
        Sf = sb(kb, es, "g_Sf", [128, 128], F32)
        bSf = Buf()
        Sb = [sb(kb, es, "g_Sb%d" % i, [128, 128], BF16) for i in range(2)]
        bSb = [Buf() for _ in range(2)]
        scm = [sb(kb, es, "g_scm%d" % i, [128, 128], BF16) for i in range(2)]
        bscm = [Buf() for _ in range(2)]
        sq = sb(kb, es, "g_sq", [128, TT], BF16)
        bsq = Buf()
        rstd = sb(kb, es, "g_rstd", [128, TT], F32)
        brstd = Buf()
        t1 = sb(kb, es, "g_t1", [128, TT], F32)
        bt1 = Buf()
        epsc = sb(kb, es, "g_epsc", [128, 1], F32)
        beps = Buf()
        kb.op("dve", lambda e: e.memset(epsc[:, :], EPS), writes=[beps])
        stg = [sb(kb, es, "g_stg%d" % i, [128, TT], BF16) for i in range(2)]
        bstg = [Buf() for _ in range(2)]
        sps = [ps(kb, es, "g_sps%d" % i, [128, 128], F32) for i in range(2)]
        bsps = [Buf() for _ in range(2)]
        sp2 = [ps(kb, es, "g_sp2%d" % i, [128, 128], F32) for i in range(2)]
        bsp2 = [Buf() for _ in range(2)]
        ops = [ps(kb, es, "g_ops%d" % i, [128, TT], F32) for i in range(2)]
        bops = [Buf() for _ in range(2)]
        ssp = ps(kb, es, "g_ssp", [128, TT], F32)
        bssp = Buf()
        for hd in range(2):
            r0 = 128 * hd
            TOK = cfg["TOK"]
            rdz = [fx.bZFr, fx.bZTr]
            first_w = True
            for t in range(2):
                cs = slice(t * TOK, (t + 1) * TOK)
                for til, nm in ((qT, "gqT"), (kT, "gkT"), (rT, "grT")):
                    kb.dma("sp", til[:, cs], fx.b_fm(nm, r0, 128, t, 0, TOK), reads=rdz, writes=[bin_], add_write=not first_w)
                    first_w = False
                for til, nm in ((kh, "gkh"), (v, "gv")):
                    kb.dma("sp", til[:, t * NCH:(t + 1) * NCH, :], fx.b_tm(nm, t, 0, TOK, r0, 128).rearrange("(c p) d -> p c d", p=128),
                           reads=rdz, writes=[bin_], add_write=True)
            kb.op("dve", lambda e: e.memset(Sf[:, :], 0.0), writes=[bSf])
            kb.op("dve", lambda e: e.memset(Sb[0][:, :], 0.0), writes=[bSb[0]])
            for c in range(NB):
                blk = slice(128 * c, 128 * c + 128)
                sp_, bsp_ = sps[c % 2], bsps[c % 2]
                kb.op("pe", lambda e, sp_=sp_, blk=blk: e.matmul(sp_[:, :], lhsT=kT[:, blk], rhs=qT[:, blk], start=True, stop=True),
                      reads=[bin_], writes=[bsp_])
                sc_, bsc_ = scm[c % 2], bscm[c % 2]
                kb.op("dve", lambda e, sp_=sp_, sc_=sc_: e.tensor_tensor(out=sc_[:, :], in0=sp_[:, :], in1=gmask[:, :], op=ALU.mult),
                      reads=[bsp_, bgm], writes=[bsc_])
                op_, bop_ = ops[(c // 4) % 2], bops[(c // 4) % 2]
                oc = slice(128 * (c % 4), 128 * (c % 4) + 128)

                def f(e, op_=op_, oc=oc, blk=blk, c=c, sc_=sc_):
                    e.matmul(op_[:, oc], lhsT=Sb[c % 2][:, :], rhs=qT[:, blk], start=True, stop=False)
                    return e.matmul(op_[:, oc], lhsT=v[:, c, :], rhs=sc_[:, :], start=False, stop=True)
                kb.op("pe", f, reads=[bSb[c % 2], bin_, bsc_], writes=[bop_])
                s2_, bs2_ = sp2[c % 2], bsp2[c % 2]
                kb.op("pe", lambda e, s2_=s2_, c=c: e.matmul(s2_[:, :], lhsT=kh[:, c, :], rhs=v[:, c, :], start=True, stop=True),
                      reads=[bin_], writes=[bs2_])
                kb.op("dve", lambda e, s2_=s2_, c=c, hd=hd: e.scalar_tensor_tensor(out=Sf[:, :], in0=Sf[:, :], scalar=eb[:, hd, c:c + 1], in1=s2_[:, :],
                                                                                 op0=ALU.mult, op1=ALU.add),
                      reads=[bs2_, beb, bSf], writes=[bSf])
                kb.op("act", lambda e, c=c: e.activation(out=Sb[(c + 1) % 2][:, :], in_=Sf[:, :], func=AF.Copy),
                      reads=[bSf], writes=[bSb[(c + 1) % 2]])
                if c % 4 == 3:
                    tg = c // 4
                    kb.op("act", lambda e, op_=op_: e.activation(out=sq[:, :], in_=op_[:, :], func=AF.Square), reads=[bop_], writes=[bsq])
                    kb.op("pe", lambda e: e.matmul(ssp[:, :], lhsT=ones[:, :], rhs=sq[:, :], start=True, stop=True), reads=[bsq, bones], writes=[bssp])
                    kb.op("act", lambda e: e.activation(out=rstd[:, :], in_=ssp[:, :], func=AF.Sqrt, scale=1.0 / 128, bias=epsc[:, 0:1]),
                          reads=[bssp, beps], writes=[brstd])
                    kb.op("dve", lambda e: e.reciprocal(out=rstd[:, :], in_=rstd[:, :]), reads=[brstd], writes=[brstd])
                    kb.op("dve", lambda e, op_=op_: e.tensor_tensor(out=t1[:, :], in0=op_[:, :], in1=rstd[:, :], op=ALU.mult),
                          reads=[bop_, brstd], writes=[bt1])
                    s_, bs_ = stg[tg % 2], bstg[tg % 2]
                    kb.op("dve", lambda e, s_=s_, tg=tg: e.scalar_tensor_tensor(out=s_[:, :], in0=t1[:, :], scalar=gog[:, 0:1], in1=rT[:, TT * tg:TT * tg + TT],
                                                                                op0=ALU.mult, op1=ALU.mult),
                          reads=[bt1, bgog, bin_], writes=[bs_])
                    kb.dma("pool", fx.b_y(768 + r0, TT * tg), s_[:, :], reads=[bs_], writes=[fx.bWw], add_write=True)
        kb.barrier()


def emit_phaseB_sb(kb, dr, cfg, C, fx):
    S = cfg["S"]
    NB = S // 128
    ident, bid = C["ident"]
    sbmask, bmask = C["sbmask"]
    zeros, bzer = C["zeros"]
    LAG = 2
    with ExitStack() as es:
        qT = [sb(kb, es, "s_qT%d" % i, [128, S], BF16) for i in range(2)]
        kT = [sb(kb, es, "s_kT%d" % i, [128, S], BF16) for i in range(2)]
        vv = [sb(kb, es, "s_v%d" % i, [128, NB, 128], BF16) for i in range(2)]
        bin_ = [Buf() for _ in range(2)]
        NZ, NA = 3, 4
        zps = [ps(kb, es, "s_zps%d" % i, [128, TT], F32) for i in range(NZ)]
        bzps = [Buf() for _ in range(NZ)]
        tps = [ps(kb, es, "s_tps%d" % i, [128, TT], F32) for i in range(2)]
        btps = [Buf() for _ in range(2)]
        ops = [ps(kb, es, "s_ops%d" % i, [128, 128], F32) for i in range(2)]
        bops = [Buf() for _ in range(2)]
        Bt = [sb(kb, es, "s_Bt%d" % i, [128, TT], F32) for i in range(2)]
        bBt = [Buf() for _ in range(2)]
        KBf = [sb(kb, es, "s_KB%d" % i, [128, TT + 1], F32) for i in range(2)]
        bKB = [Buf() for _ in range(2)]
        PX = [sb(kb, es, "s_PX%d" % i, [128, TT + 1], F32) for i in range(3)]
        bPX = [Buf() for _ in range(3)]
        A = [sb(kb, es, "s_A%d" % i, [128, TT], BF16) for i in range(NA)]
        bA = [Buf() for _ in range(NA)]
        AT = [sb(kb, es, "s_AT%d" % i, [128, TT], BF16) for i in range(2)]
        bAT = [Buf() for _ in range(2)]
        ost = [sb(kb, es, "s_ost%d" % i, [128, TT], BF16) for i in range(2)]
        bost = [Buf() for _ in range(2)]
        for i in range(2):
            kb.op("dve", lambda e, i=i: e.memset(KBf[i][:, 0:1], 1.0), writes=[bKB[i]])
        tiles = []
        for hd in range(4):
            for i in range(NB):
                nkb = i + 1
                rb0 = NB - 1 - i
                nt = (nkb + 3) // 4
                for m in range(nt):
                    tiles.append((hd, i, m, rb0 + 4 * m, min(4, nkb - 4 * m), m == 0, m == nt - 1))
        NTI = len(tiles)

        def load_head(hd):
            s_ = hd % 2
            r0 = 128 * hd
            TOK = cfg["TOK"]
            NCH = NB // 2
            rdz = [fx.bZFr, fx.bZTr]
            for t in range(2):
                kb.dma("sp", qT[s_][:, t * TOK:(t + 1) * TOK], fx.b_fm("sbqT", r0, 128, t, 0, TOK), reads=rdz, writes=[bin_[s_]], add_write=(t == 1))
                kb.dma("sp", kT[s_][:, (1 - t) * TOK:(2 - t) * TOK], fx.b_fm("sbkT", r0, 128, t, 0, TOK), reads=rdz, writes=[bin_[s_]], add_write=True)
                kb.dma("sp", vv[s_][:, (1 - t) * NCH:(2 - t) * NCH, :], fx.b_tm("sbv", t, 0, TOK, r0, 128).rearrange("(c p) d -> p c d", p=128),
                       reads=rdz, writes=[bin_[s_]], add_write=True)

        def front(n):
            hd, i, m, rb, nb, first, last = tiles[n]
            s_ = hd % 2
            w = 128 * nb
            z, bz = zps[n % NZ], bzps[n % NZ]

            def f(e):
                ins = e.matmul(z[:, 0:w], lhsT=qT[s_][:, 128 * i:128 * i + 128], rhs=kT[s_][:, 128 * rb:128 * rb + w], start=True, stop=not first)
                if first:
                    ins = e.matmul(z[:, 0:128], lhsT=ident[:, :], rhs=sbmask[:, :], start=False, stop=True)
                return ins
            kb.op("pe", f, reads=[bin_[s_], bid, bmask], writes=[bz])
            b_, bb_ = Bt[n % 2], bBt[n % 2]
            k_, bk_ = KBf[n % 2], bKB[n % 2]
            kb.op("act", lambda e: e.activation(out=b_[:, 0:w], in_=z[:, 0:w], func=AF.Sigmoid), reads=[bz], writes=[bb_])
            kb.op("act", lambda e: e.activation(out=k_[:, 1:w + 1], in_=z[:, 0:w], func=AF.Sigmoid, scale=-1.0), reads=[bz], writes=[bk_])
            px, bpx = PX[n % 3], bPX[n % 3]
            if first:
                kb.op("dve", lambda e: e.tensor_tensor_scan(out=px[:, 0:w + 1], data0=k_[:, 0:w + 1], data1=zeros[:, 0:w + 1], initial=1.0,
                                                            op0=ALU.mult, op1=ALU.add), reads=[bk_, bzer], writes=[bpx])
            else:
                ppx, bppx = PX[(n - 1) % 3], bPX[(n - 1) % 3]
                kb.op("dve", lambda e: e.tensor_tensor_scan(out=px[:, 0:w + 1], data0=k_[:, 0:w + 1], data1=zeros[:, 0:w + 1],
                                                            initial=ppx[:, TT:TT + 1], op0=ALU.mult, op1=ALU.add),
                      reads=[bk_, bzer, bppx], writes=[bpx])
            a_, ba_ = A[n % NA], bA[n % NA]
            kb.op("pool", lambda e: e.tensor_tensor(out=a_[:, 0:w], in0=b_[:, 0:w], in1=px[:, 0:w], op=ALU.mult), reads=[bb_, bpx], writes=[ba_])

        def back(n):
            hd, i, m, rb, nb, first, last = tiles[n]
            s_ = hd % 2
            w = 128 * nb
            a_, ba_ = A[n % NA], bA[n % NA]
            tp, btp = tps[n % 2], btps[n % 2]

            def f(e):
                ins = None
                for j in range(nb):
                    ins = e.matmul(tp[:, 128 * j:128 * j + 128], lhsT=a_[:, 128 * j:128 * j + 128], rhs=ident[:, :], start=True, stop=True)
                return ins
            kb.op("pe", f, reads=[ba_, bid], writes=[btp])
            at, bat = AT[n % 2], bAT[n % 2]
            if n % 2 == 0:
                kb.op("act", lambda e: e.activation(out=at[:, 0:w], in_=tp[:, 0:w], func=AF.Copy), reads=[btp], writes=[bat])
            else:
                kb.op("dve", lambda e: e.tensor_copy(out=at[:, 0:w], in_=tp[:, 0:w]), reads=[btp], writes=[bat])
            qi = hd * NB + i
            o, bo = ops[qi % 2], bops[qi % 2]

            def g(e):
                ins = None
                for j in range(nb):
                    ins = e.matmul(o[:, :], lhsT=vv[s_][:, rb + j, :], rhs=at[:, 128 * j:128 * j + 128], start=(first and j == 0), stop=(last and j == nb - 1))
                return ins
            kb.op("pe", g, reads=[bat, bin_[s_]], writes=[bo])
            if last:
                os_, bos_ = ost[(qi // 4) % 2], bost[(qi // 4) % 2]
                kb.op("act", lambda e: e.activation(out=os_[:, 128 * (i % 4):128 * (i % 4) + 128], in_=o[:, :], func=AF.Copy), reads=[bo], writes=[bos_])
                if i % 4 == 3:
                    kb.dma("pool", fx.b_y(256 + 128 * hd, 128 * (i - 3)), os_[:, :], reads=[bos_], writes=[fx.bWw], add_write=True)

        load_head(0)
        loaded = 0
        for n in range(NTI + LAG):
            if n < NTI:
                hd = tiles[n][0]
                if hd > loaded:
                    loaded = hd
                if tiles[n][1] == 0 and tiles[n][2] == 0 and hd + 1 < 4:
                    pass
                front(n)
            if n - LAG >= 0:
                back(n - LAG)
                hdb, ib, mb = tiles[n - LAG][0], tiles[n - LAG][1], tiles[n - LAG][2]
                if tiles[n - LAG][6] and ib == NB - 1 and hdb + 2 < 4:
                    load_head(hdb + 2)
            if n == 0:
                load_head(1)
        kb.barrier()


class DR(dict):
    def add(self, nc, name, shape, dt, kind):
        self[name] = nc.dram_tensor(name, list(shape), dt, kind=kind).ap()
        self["_b_" + name] = Buf(name)

    def alias(self, new, old, newbuf=False):
        self[new] = self[old]
        self["_b_" + new] = Buf(new) if newbuf else self["_b_" + old]


FMB = {"uT": (0, 256), "sbqT": (256, 512), "sbkT": (768, 512), "gqT": (1280, 256), "gkT": (1536, 256), "grT": (1792, 256)}
TMB = {"sbv": (0, 512), "gkh": (512, 256), "gv": (768, 256)}


def cc_chunks(rows, cols, cc_max):
    nch = 1
    while (rows // nch) * cols * 2 > cc_max or (rows % nch) or ((rows // nch) % 128):
        nch += 1
    return nch, rows // nch


class FX:
    def __init__(self, nc, dr, cfg):
        self.dr = dr
        self.TOK = cfg["TOK"]
        self.par = nc.partition_id() % 2
        self.cc_max = cfg.get("cc_max", 2 * 1024 * 1024)
        self.nfence = cfg.get("nfence", 3)
        for n in ("ZFw", "ZFr", "ZTw", "ZTr", "ZEw", "ZEr", "Ww", "Wr"):
            setattr(self, "b" + n, Buf(n))
        self.bFN = Buf("fence")
        self.bS = {f: Buf(f + "_S") for f in ("ZF", "ZT", "W")}
        self.bR = {f: Buf(f + "_R") for f in ("ZF", "ZT", "W")}
        self.bZ = {}
        self.bM = {f: Buf(f + "_M") for f in ("ZF", "ZT", "W")}
        self.bH = {f: Buf(f + "_H") for f in ("ZF", "ZT", "W")}
        self.bT = {f: Buf(f + "_T") for f in ("ZF", "ZT", "W")}

    def v3(self, name):
        return self.dr[name].rearrange("(s r) c -> s r c", s=2)

    def a_fm(self, name, r0, nr, c0, ncw):
        base, rh = FMB[name]
        hh, loc = divmod(r0, rh)
        return self.dr["ZF_A"][hh * 2048 + base + loc:hh * 2048 + base + loc + nr, c0:c0 + ncw]

    def a_tm(self, name, r0, nr, c0, ncw):
        base, ch = TMB[name]
        hh, loc = divmod(c0, ch)
        return self.dr["ZT_A"][hh * self.TOK + r0:hh * self.TOK + r0 + nr, base + loc:base + loc + ncw]

    def b_fm(self, name, r0, nr, t, c0, ncw):
        base, rh = FMB[name]
        return self.dr["ZF_B"][t * 2048 + base + r0:t * 2048 + base + r0 + nr, c0:c0 + ncw]

    def b_tm(self, name, t, r0, nr, c0, ncw):
        base, ch = TMB[name]
        return self.dr["ZT_B"][t * self.TOK + r0:t * self.TOK + r0 + nr, base + c0:base + c0 + ncw]

    def b_y(self, row0, col0):
        t, lc = divmod(col0, self.TOK)
        return self.dr["W_A"][t * 1024 + row0:t * 1024 + row0 + 128, lc:lc + TT]

    def c_y(self, hh, yr0, nr, t0):
        return self.dr["W_B"][hh * 1024 + yr0:hh * 1024 + yr0 + nr, t0:t0 + TT]

    def exchange(self, kb, fam, groups, bw, br, q):
        dr, par = self.dr, self.par
        S = dr[fam + "_S"]
        rows, cols = S.shape[0], S.shape[1]
        nch, rc = cc_chunks(rows, cols, self.cc_max)
        A3, B3 = self.v3(fam + "_A"), self.v3(fam + "_B")
        fold = "o (p k) c -> p (o k) c"
        f2 = "(p m) k -> p m k"
        kb.dma(q, S.rearrange("(p k) c -> p k c", p=128), A3[bass.ds(1 - par, 1)].rearrange(fold, p=128),
               reads=[bw], writes=[self.bS[fam]])
        bRc = []
        self.n_x = getattr(self, "n_x", 0) + 1
        xkey = "ccx%d" % self.n_x
        self.fkey = "ccf%d" % self.n_x
        for c in range(nch):
            Sc, Rc = dr["%s_S%d" % (fam, c)], dr["%s_R%d" % (fam, c)]
            bSc, bR_ = dr["_b_%s_S%d" % (fam, c)], dr["_b_%s_R%d" % (fam, c)]
            kb.dma(q, Sc.rearrange(f2, p=128), S[c * rc:(c + 1) * rc, :].rearrange(f2, p=128), reads=[self.bS[fam]], writes=[bSc])
            tokc = kb.collective(Sc.opt(), Rc.opt(), groups, reads=[bSc], writes=[bR_], semkey=xkey)
            bRc.append(bR_)
        bAll = Buf()
        bAll.w = [tokc]
        for b_ in bRc:
            b_.w = [tokc]
        self.fence(kb, groups, bAll)
        kb.dma(q, B3[bass.ds(par, 1)].rearrange(fold, p=128), A3[bass.ds(par, 1)].rearrange(fold, p=128),
               reads=[bw], writes=[br])
        M = dr[fam + "_M"]
        M3 = self.v3(fam + "_M")
        bM = self.bM[fam]
        first = True
        for c in range(nch):
            Rc = dr["%s_R%d" % (fam, c)]
            for t in range(2):
                kb.dma(q, M[t * rows + c * rc:t * rows + (c + 1) * rc, :].rearrange(f2, p=128),
                       Rc[t * rc:(t + 1) * rc, :].rearrange(f2, p=128),
                       reads=[bRc[c], self.bFN], writes=[bM], add_write=not first)
                first = False
        kb.dma(q, B3[bass.ds(1 - par, 1)].rearrange(fold, p=128), M3[bass.ds(1 - par, 1)].rearrange(fold, p=128),
               reads=[bM], writes=[br], add_write=True)

    def fence(self, kb, groups, bdep):
        if not hasattr(self, "bFN"):
            self.bFN = Buf("fence")
        dep = bdep
        self.n_f = getattr(self, "n_f", 0) + 1
        fk = "ccf%d" % self.n_f
        for rep in range(self.nfence):
            kb.collective(self.dr["FN_S"].opt(), self.dr["FN_Q"].opt(), groups, reads=[dep], writes=[self.bFN], kind="AllReduce", semkey=fk)
            kb.collective(self.dr["FN_S"].opt(), self.dr["FN_R"].opt(), groups, reads=[self.bFN], writes=[self.bFN], semkey=fk)
            dep = self.bFN


WEIGHTS = {"w_in": (D, INC), "w_br_pool": (512, D), "w_br_sb": (1024, D), "w_br_gla": (512, D), "w_out": (D, D),
           "w_ff_gate": (D, DFF), "w_ff_up": (D, DFF), "w_ff_down": (DFF, D)}
WBF = {"w_in_bf": (D, INC), "w_br_bf": (D, D), "w_out_bf": (D, D), "w_gate_bf": (D, DFF), "w_up_bf": (D, DFF), "w_down_bf": (DFF, D)}


def build_fused(cfg, depth):
    TOK, S, NCORE = cfg["TOK"], cfg["S"], cfg["NCORE"]
    NCH = TOK // 128
    groups = [[2 * i, 2 * i + 1] for i in range(NCORE // 2)]
    nc = bass.Bass("TRN2", target_bir_lowering=False)
    dr = DR()
    for n, (shp, dt) in CONST_SHARED.items():
        dr.add(nc, n, shp, dt, "ExternalInput")
    dr.add(nc, "c_col", [128, KD], F32, "ExternalInput")
    dr.add(nc, "x_tok", [TOK, D], F32, "ExternalInput")
    dr.add(nc, "out", [TOK, D], F32, "ExternalOutput")
    for l in range(depth):
        for n, (shp, dt) in CONST_LAYER.items():
            dr.add(nc, "%s_%d" % (n, l), shp, dt, "ExternalInput")
        dr.add(nc, "b_ada%d" % l, [128, 96], F32, "ExternalInput")
        dr.add(nc, "g_norm1_%d" % l, [128, KD], F32, "ExternalInput")
        dr.add(nc, "g_norm2_%d" % l, [128, KD], F32, "ExternalInput")
        dr.add(nc, "w_ada%d" % l, [D, 6 * D], F32, "ExternalInput")
        for n, shp in WEIGHTS.items():
            dr.add(nc, "%s_%d" % (n, l), shp, F32, "ExternalInput")
        for n, shp in WBF.items():
            dr.add(nc, "%s_%d" % (n, l), shp, BF16, "Internal")
        dr.alias("w_in_bfA_%d" % l, "w_in_bf_%d" % l, newbuf=True)
    dr.add(nc, "X0", [D, TOK], F32, "Internal")
    dr.add(nc, "X1", [D, TOK], F32, "Internal")
    for fam, (rr, cc_) in (("ZF", (2048, TOK)), ("ZT", (TOK, 1024)), ("W", (1024, TOK))):
        dr.add(nc, fam + "_A", [2 * rr, cc_], BF16, "Internal")
        dr.add(nc, fam + "_S", [rr, cc_], BF16, "Internal")
        dr.add(nc, fam + "_M", [2 * rr, cc_], BF16, "Internal")
        nch_, rc_ = cc_chunks(rr, cc_, cfg.get("cc_max", 2 * 1024 * 1024))
        for c_ in range(nch_):
            dr.add(nc, "%s_S%d" % (fam, c_), [rc_, cc_], BF16, "Internal")
            dr.add(nc, "%s_R%d" % (fam, c_), [2 * rc_, cc_], BF16, "Internal")
        dr.add(nc, fam + "_B", [2 * rr, cc_], BF16, "Internal")
    dr.add(nc, "FN_S", [128, 16], F32, "Internal")
    dr.add(nc, "FN_R", [256, 16], F32, "Internal")
    dr.add(nc, "FN_Q", [128, 16], F32, "Internal")
    dr.add(nc, "ZE_A", [128, 4 * NCH], F32, "Internal")
    dr.add(nc, "ZE_R", [2 * 128, 4 * NCH], F32, "Internal")
    with ExitStack() as es:
        kb = KB(nc, es)
        fx = FX(nc, dr, cfg)
        CS = emit_consts(kb, es, dr, CONST_SHARED)
        CL = [emit_consts(kb, es, dr, CONST_LAYER, "_%d" % l) for l in range(depth)]
        Ms = [None] * depth
        Ms[0] = emit_mod(kb, es, dr, 0, CS)
        for l in range(depth):
            sfx = "_%d" % l
            emit_wcast(kb, dr, "w_in" + sfx, "w_in_bfA" + sfx, D, 0, C_GATE)
            emit_wcast(kb, dr, "w_in" + sfx, "w_in_bf" + sfx, D, C_GATE, INC)
            emit_wcast(kb, dr, "w_br_pool" + sfx, "w_br_bf" + sfx, 512, 0, D, r_dst0=0)
            emit_wcast(kb, dr, "w_br_sb" + sfx, "w_br_bf" + sfx, 1024, 0, D, r_dst0=512, first=False)
            emit_wcast(kb, dr, "w_br_gla" + sfx, "w_br_bf" + sfx, 512, 0, D, r_dst0=1536, first=False)
            emit_wcast(kb, dr, "w_out" + sfx, "w_out_bf" + sfx, D, 0, D)
            emit_wcast(kb, dr, "w_ff_gate" + sfx, "w_gate_bf" + sfx, D, 0, DFF)
            emit_wcast(kb, dr, "w_ff_up" + sfx, "w_up_bf" + sfx, D, 0, DFF)
            emit_wcast(kb, dr, "w_ff_down" + sfx, "w_down_bf" + sfx, DFF, 0, D)
        for l in range(depth):
            sfx = "_%d" % l
            C = dict(CS)
            C.update(CL[l])
            last = (l == depth - 1)
            dr.alias("w_in_bf", "w_in_bfA" + sfx)
            if l == 0:
                dr.alias("xT_out", "X0")
            else:
                dr.alias("xT", "X%d" % ((l - 1) % 2 + 0) if False else ("X1" if l % 2 == 1 else "X0"))
            emit_phaseA(kb, dr, l, cfg, C, Ms[l], l == 0, fx)
            if cfg.get("stop") == "%d:A" % l:
                break
            fx.exchange(kb, "ZF", groups, fx.bZFw, fx.bZFr, "sp")
            fx.exchange(kb, "ZT", groups, fx.bZTw, fx.bZTr, "act")
            kb.collective(dr["ZE_A"].opt(), dr["ZE_R"].opt(), groups, reads=[fx.bZEw], writes=[fx.bZEr])
            fx.fence(kb, groups, fx.bZEr)
            if cfg.get("stop") == "%d:X1" % l:
                break
            if l + 1 < depth:
                Ms[l + 1] = emit_mod(kb, es, dr, l + 1, CS)
            emit_phaseB_pool(kb, dr, cfg, C, fx)
            emit_phaseB_gla(kb, dr, cfg, C, fx)
            emit_phaseB_sb(kb, dr, cfg, C, fx)
            if cfg.get("stop") == "%d:B" % l:
                break
            fx.exchange(kb, "W", groups, fx.bWw, fx.bWr, "sp")
            if cfg.get("stop") == "%d:X2" % l:
                break
            dr.alias("w_in_bf", "w_in_bf" + sfx)
            for n in ("w_br_bf", "w_out_bf", "w_gate_bf", "w_up_bf", "w_down_bf"):
                dr.alias(n, n + sfx)
            dr.alias("xT", "X%d" % (l % 2))
            if not last:
                dr.alias("xT_out", "X%d" % ((l + 1) % 2))
            emit_phaseC(kb, dr, l, cfg, C, Ms[l], last, fx)
            if cfg.get("stop") == "%d:C" % l:
                break
        kb.finish([dr["_b_out"]])
        n_inst = kb.n_inst
    return nc, n_inst


POOL_WINDOWS = (2, 4, 8, 16)


def host_consts_shared(h):
    c = {}
    c["ident"] = np.eye(128, dtype=np.float32).astype(NPBF)
    c["ones"] = np.ones((128, 128), np.float32).astype(NPBF)
    c["identf"] = np.eye(128, dtype=np.float32)
    c["jrev"] = np.eye(128, dtype=np.float32)[::-1].copy().astype(NPBF)
    rs = np.ones((128, TT), np.float32)
    rs[:, ::128] = 0.0
    c["reset"] = rs
    p = np.arange(128)[:, None]
    cc = np.arange(128)[None, :]
    c["sbmask"] = np.where(cc + p > 127, 0.0, NEG).astype(np.float32).astype(NPBF)
    c["glamask"] = (p <= cc).astype(np.float32)
    c["zeros"] = np.zeros((128, TT + 1), np.float32)
    pc = np.zeros((128, 2, 4), np.float32)
    pi = np.ones((128, 2, 16), np.float32)
    for gi in range(2):
        w = POOL_WINDOWS[2 * h + gi]
        lv = {2: 0, 4: 1, 8: 2, 16: 3}[w]
        pc[:, gi, lv] = 1.0 / w
        for t in range(16):
            pi[:, gi, t] = float(w) / float(min(t + 1, w))
    c["pcoef"] = pc
    c["pinv"] = pi
    hs = np.zeros((128, 2), np.float32)
    hs[:, h] = 1.0
    c["hsel"] = hs
    return c


def host_consts_layer(inp, l, h):
    c = {}
    c["gq"] = np.ascontiguousarray(inp["sb_q_gain"][l][:, None])
    c["gk"] = np.ascontiguousarray(inp["sb_k_gain"][l][:, None])
    c["wa2"] = np.ascontiguousarray(inp["gla_w_a2"][l])
    c["ba2"] = np.ascontiguousarray(inp["gla_b_a2"][l].reshape(4, 128).T)
    c["gog"] = np.ascontiguousarray(inp["gla_out_gain"][l][:, None])
    wp = inp["w_pool"][l][2 * h:2 * h + 2]
    c["wpool"] = np.ascontiguousarray(wp.transpose(1, 0, 2))
    c["pscale"] = np.ascontiguousarray(inp["pool_scale"][l][2 * h:2 * h + 2].T)
    return {"%s_%d" % (k, l): v for k, v in c.items()}


def run_module(inp, B, S):
    inp = {k: np.asarray(v) for k, v in inp.items()}
    TOK = S // 2
    NCORE = 2 * B
    cfg = {"S": S, "TOK": TOK, "NCORE": NCORE}
    cfg.update(getattr(run_module, "extra_cfg", {}))
    cores = list(range(NCORE))
    depth = inp["w_in"].shape[0]
    nc, n_inst = build_fused(cfg, depth)
    shared_l = {}
    for l in range(depth):
        shared_l["b_ada%d" % l] = np.ascontiguousarray(inp["b_ada"][l].reshape(96, 128).T)
        shared_l["g_norm1_%d" % l] = np.ascontiguousarray(inp["g_norm1"][l].reshape(KD, 128).T)
        shared_l["g_norm2_%d" % l] = np.ascontiguousarray(inp["g_norm2"][l].reshape(KD, 128).T)
        shared_l["w_ada%d" % l] = inp["w_ada"][l]
        for n in WEIGHTS:
            shared_l["%s_%d" % (n, l)] = inp[n][l]
    maps = []
    for c in cores:
        b, h = c // 2, c % 2
        m = dict(shared_l)
        m.update(host_consts_shared(h))
        for l in range(depth):
            m.update(host_consts_layer(inp, l, h))
        m["c_col"] = np.ascontiguousarray(inp["c"][b].reshape(KD, 128).T)
        m["x_tok"] = np.ascontiguousarray(inp["x"][b, h * TOK:(h + 1) * TOK])
        maps.append(m)
    res = run_bass_kernel_spmd(nc, maps, core_ids=cores).results
    out = np.zeros((B, S, D), np.float32)
    for c in cores:
        b, h = c // 2, c % 2
        out[b, h * TOK:(h + 1) * TOK] = res[c]["out"]
    return out


def kernel(**inputs):
    return run_module(inputs, 4, 8192)
```
